# Optimizing a Trainium2 kernel written in Bass

```python
import math
import jax, jax.numpy as jnp
from jax import lax
import numpy as np

D_MODEL = 1024
BATCH = 16
SEQ = 2048
DEPTH = 4

CTX_LEN = 256
GRID_W = 64
NORM_EPS = 1e-6
HEAD_DIM = 64

ATTN_WIDTH = D_MODEL // 2
ATTN_HEADS = ATTN_WIDTH // HEAD_DIM
ATTN_KV_HEADS = 2
GQA_GROUP = ATTN_HEADS // ATTN_KV_HEADS
ATTN_KV_WIDTH = ATTN_KV_HEADS * HEAD_DIM
ROPE_THETA = 10000.0
Q_BLOCK = 128

GDN_WIDTH = D_MODEL // 4
GDN_DK = 64
GDN_DV = 64
GDN_HEADS = GDN_WIDTH // GDN_DV
GDN_KEY_WIDTH = GDN_HEADS * GDN_DK
GDN_QKV_WIDTH = 2 * GDN_KEY_WIDTH + GDN_WIDTH
GDN_CHUNK = 64
SHORT_CONV = 3

HYENA_WIDTH = D_MODEL // 4
HYENA_ORDER = 2
HYENA_EMB_BANDS = 16
HYENA_EMB_DIM = 1 + 2 * HYENA_EMB_BANDS
HYENA_FILTER_HIDDEN = 64
HYENA_FAST_DECAY = 0.3
HYENA_SLOW_DECAY = 1.5
HYENA_DECAY_TARGET = 1e-2
HYENA_CONV = 3

MIX_WIDTH = ATTN_WIDTH + GDN_WIDTH + HYENA_WIDTH
IN_SPLITS = (ATTN_WIDTH, ATTN_KV_WIDTH, ATTN_KV_WIDTH,
             GDN_QKV_WIDTH, GDN_WIDTH, 2 * GDN_HEADS, 2 * GDN_HEADS,
             (HYENA_ORDER + 1) * HYENA_WIDTH)
IN_WIDTH = sum(IN_SPLITS)

D_FF = 2816
FFN_CONV = 3

kernel_name = 'hybrid_gqa_gdn_hyena_prefix_dit'


def rms_norm(x, gain):
    xf = x.astype(jnp.float32)
    y = xf * lax.rsqrt(jnp.mean(xf * xf, axis=-1, keepdims=True) + NORM_EPS)
    return (y * gain.astype(jnp.float32)).astype(x.dtype)


def l2_normalize(x):
    xf = x.astype(jnp.float32)
    return xf * lax.rsqrt(jnp.sum(xf * xf, axis=-1, keepdims=True) + NORM_EPS)


def dw_conv(x, w):
    return lax.conv_general_dilated(x, w[:, None, :].astype(x.dtype), window_strides=(1,), padding='SAME',
                                    dimension_numbers=('NWC', 'WIO', 'NWC'), feature_group_count=x.shape[-1])


def rope_1d(x, pos):
    d = x.shape[-1]
    inv = ROPE_THETA ** (-jnp.arange(d // 2, dtype=jnp.float32) / (d // 2))
    ang = pos.astype(jnp.float32)[:, None] * inv[None, :]
    cos = jnp.cos(ang)[None, :, None, :]
    sin = jnp.sin(ang)[None, :, None, :]
    x1, x2 = x[..., :d // 2], x[..., d // 2:]
    return jnp.concatenate([x1 * cos - x2 * sin, x2 * cos + x1 * sin], axis=-1)


def axial_rope(x):
    n_tok = x.shape[1]
    rows = n_tok // GRID_W
    row_id = jnp.repeat(jnp.arange(rows), GRID_W)
    col_id = jnp.tile(jnp.arange(GRID_W), rows)
    half = x.shape[-1] // 2
    xf = x.astype(jnp.float32)
    out = jnp.concatenate([rope_1d(xf[..., :half], row_id), rope_1d(xf[..., half:], col_id)], axis=-1)
    return out.astype(x.dtype)


def gqa_attend(q, k, v):
    s = jnp.einsum('bqkgd,bskd->bkgqs', q, k).astype(jnp.float32) * (HEAD_DIM ** -0.5)
    p = jax.nn.softmax(s, axis=-1).astype(v.dtype)
    return jnp.einsum('bkgqs,bskd->bqkgd', p, v)


def attention_mixer(qkv_c, qkv_l, q_gain, k_gain, with_ctx):
    def heads(t, n):
        return t.reshape(t.shape[0], t.shape[1], n, HEAD_DIM)
    q_c, k_c, v_c = qkv_c
    q_l, k_l, v_l = qkv_l
    b, n_lat = q_l.shape[:2]
    kc = rms_norm(heads(k_c, ATTN_KV_HEADS), k_gain)
    vc = heads(v_c, ATTN_KV_HEADS)
    ql = axial_rope(rms_norm(heads(q_l, ATTN_HEADS), q_gain))
    kl = axial_rope(rms_norm(heads(k_l, ATTN_KV_HEADS), k_gain))
    vl = heads(v_l, ATTN_KV_HEADS)
    k_all = jnp.concatenate([kc, kl], axis=1)
    v_all = jnp.concatenate([vc, vl], axis=1)
    qb = ql.reshape(b, n_lat // Q_BLOCK, Q_BLOCK, ATTN_KV_HEADS, GQA_GROUP, HEAD_DIM)
    out_l = lax.map(lambda q: gqa_attend(q, k_all, v_all), jnp.moveaxis(qb, 1, 0))
    out_l = jnp.moveaxis(out_l, 0, 1).reshape(b, n_lat, ATTN_WIDTH)
    out_c = None
    if with_ctx:
        n_ctx = q_c.shape[1]
        qc = rms_norm(heads(q_c, ATTN_HEADS), q_gain).reshape(b, n_ctx, ATTN_KV_HEADS, GQA_GROUP, HEAD_DIM)
        out_c = gqa_attend(qc, kc, vc).reshape(b, n_ctx, ATTN_WIDTH)
    return out_c, out_l


def gated_delta_chunked(q, k, v, g, beta, state0):
    b, n_tok, h, dk = k.shape
    dv = v.shape[-1]
    c = GDN_CHUNK
    n = n_tok // c

    def to_chunks(t):
        t = t.astype(jnp.float32).reshape((b, n, c) + t.shape[2:])
        return jnp.moveaxis(t, 3, 2)
    q = to_chunks(q) * (dk ** -0.5)
    k, v, g, beta = (to_chunks(t) for t in (k, v, g, beta))
    g = jnp.cumsum(g, axis=-1)
    lower = jnp.tril(jnp.ones((c, c), dtype=bool))
    diff = g[..., :, None] - g[..., None, :]
    decay = jnp.where(lower, jnp.exp(jnp.where(lower, diff, 0.0)), 0.0)
    k_beta = k * beta[..., None]
    kk = jnp.einsum('bnhcd,bnhsd->bnhcs', k_beta, k) * decay
    t_mat = jnp.eye(c, dtype=jnp.float32) + jnp.tril(kk, -1)
    rhs = jnp.concatenate([v * beta[..., None], k_beta * jnp.exp(g)[..., None]], axis=-1)
    sol = lax.linalg.triangular_solve(t_mat, rhs, left_side=True, lower=True, unit_diagonal=True)
    u, w = sol[..., :dv], sol[..., dv:]
    qk = jnp.einsum('bnhcd,bnhsd->bnhcs', q, k) * decay
    q_dec = q * jnp.exp(g)[..., None]
    k_dec = k * jnp.exp(g[..., -1:] - g)[..., None]
    g_end = jnp.exp(g[..., -1])

    def step(state, xs):
        u_i, w_i, qk_i, qd_i, kd_i, ge_i = xs
        v_new = u_i - jnp.einsum('bhcd,bhde->bhce', w_i, state)
        o = jnp.einsum('bhcd,bhde->bhce', qd_i, state) + jnp.einsum('bhcs,bhse->bhce', qk_i, v_new)
        state = state * ge_i[..., None, None] + jnp.einsum('bhcd,bhce->bhde', kd_i, v_new)
        return state, o
    xs = tuple(jnp.moveaxis(t, 1, 0) for t in (u, w, qk, q_dec, k_dec, g_end))
    state, o = lax.scan(step, state0.astype(jnp.float32), xs)
    o = jnp.moveaxis(o, 0, 1)
    o = jnp.moveaxis(o, 2, 3).reshape(b, n_tok, h, dv)
    return o, state


def gdn_mixer(p_c, p_l, conv_w, a_log, dt_bias, norm_gain, with_ctx):
    def prep(qkv, beta_logit, decay_logit):
        b, n, _ = qkv.shape
        qkv = jax.nn.silu(dw_conv(qkv, conv_w))
        q, k, v = jnp.split(qkv, [GDN_KEY_WIDTH, 2 * GDN_KEY_WIDTH], axis=-1)
        q = l2_normalize(q.reshape(b, n, GDN_HEADS, GDN_DK))
        k = l2_normalize(k.reshape(b, n, GDN_HEADS, GDN_DK))
        v = v.reshape(b, n, GDN_HEADS, GDN_DV)
        beta = jax.nn.sigmoid(beta_logit.astype(jnp.float32)).reshape(b, n, 2, GDN_HEADS)
        g = -jnp.exp(a_log.astype(jnp.float32)) * jax.nn.softplus(
            decay_logit.astype(jnp.float32).reshape(b, n, 2, GDN_HEADS) + dt_bias.astype(jnp.float32))
        return q, k, v, g, beta
    qkv_c, gate_c, beta_c, dec_c = p_c
    qkv_l, gate_l, beta_l, dec_l = p_l
    ctx_in = prep(qkv_c, beta_c, dec_c)
    lat_in = prep(qkv_l, beta_l, dec_l)
    b = qkv_l.shape[0]
    outs_c, outs_l = [], []
    for d in range(2):
        orient = (lambda t: jnp.flip(t, axis=1)) if d == 1 else (lambda t: t)

        def dir_inputs(inp):
            q, k, v, g, beta = inp
            return orient(q), orient(k), orient(v), orient(g[:, :, d]), orient(beta[:, :, d])
        s0 = jnp.zeros((b, GDN_HEADS, GDN_DK, GDN_DV), jnp.float32)
        o_c, s_ctx = gated_delta_chunked(*dir_inputs(ctx_in), s0)
        o_l, _ = gated_delta_chunked(*dir_inputs(lat_in), s_ctx)
        outs_c.append(orient(o_c))
        outs_l.append(orient(o_l))

    def finish(o, gate):
        bb, n = gate.shape[:2]
        y = rms_norm(o, norm_gain) * jax.nn.silu(gate.astype(jnp.float32)).reshape(bb, n, GDN_HEADS, GDN_DV)
        return y.reshape(bb, n, GDN_WIDTH).astype(gate.dtype)
    out_l = finish(outs_l[0] + outs_l[1], gate_l)
    out_c = finish(outs_c[0] + outs_c[1], gate_c) if with_ctx else None
    return out_c, out_l


def hyena_filter_bank(n_tok, w1, b1, w2, b2, w3, freq):
    f32 = jnp.float32
    t = jnp.linspace(0.0, 1.0, n_tok, dtype=f32)[:, None]
    omega = 2.0 * math.pi * jnp.arange(n_tok, dtype=f32)[:, None] / n_tok
    bands = jnp.linspace(1e-4, HYENA_EMB_BANDS - 1, HYENA_EMB_BANDS, dtype=f32)[None, :]
    feats = jnp.concatenate([t, jnp.cos(bands * omega), -jnp.sin(bands * omega)], axis=-1)
    fr = freq.astype(f32)
    hid = jnp.sin(fr * (feats @ w1.astype(f32) + b1.astype(f32)))
    hid = jnp.sin(fr * (hid @ w2.astype(f32) + b2.astype(f32)))
    filt = (hid @ w3.astype(f32)).reshape(n_tok, HYENA_ORDER - 1, 2, HYENA_WIDTH)
    max_decay = math.log(HYENA_DECAY_TARGET) / HYENA_FAST_DECAY
    min_decay = math.log(HYENA_DECAY_TARGET) / HYENA_SLOW_DECAY
    deltas = jnp.abs(jnp.linspace(min_decay, max_decay, HYENA_WIDTH, dtype=f32))
    window = jnp.exp(-t * deltas[None, :])
    return filt * window[:, None, None, :]


def long_conv(u, h_fwd, h_bwd, bias):
    n_tok, width = h_fwd.shape
    n_fft = 2 * n_tok
    h = jnp.concatenate([h_fwd, jnp.zeros((1, width), jnp.float32), h_bwd[:0:-1]], axis=0)
    uf = u.astype(jnp.float32)
    y = jnp.fft.irfft(jnp.fft.rfft(uf, n=n_fft, axis=1) * jnp.fft.rfft(h, axis=0)[None], n=n_fft, axis=1)[:, :n_tok]
    return (y + uf * bias.astype(jnp.float32)).astype(u.dtype)


def hyena_mixer(p, conv_w, filt, bias):
    streams = jnp.split(dw_conv(p, conv_w), HYENA_ORDER + 1, axis=-1)
    gates, v = streams[:-1], streams[-1]
    for o, gate in enumerate(reversed(gates[1:])):
        v = long_conv(v * gate, filt[:, o, 0], filt[:, o, 1], bias[o])
    return v * gates[0]


def split_cols(p):
    idx = [int(i) for i in np.cumsum(IN_SPLITS)[:-1]]
    return jnp.split(p, idx, axis=-1)


def token_mixers(p_c, p_l, q_gain, k_gain, gdn_conv, gdn_a_log, gdn_dt_bias, gdn_norm,
                 hyena_conv, hw1, hb1, hw2, hb2, hw3, hfreq, hbias, with_ctx):
    cc = split_cols(p_c)
    cl = split_cols(p_l)
    a_c, a_l = attention_mixer(cc[0:3], cl[0:3], q_gain, k_gain, with_ctx)
    g_c, g_l = gdn_mixer(cc[3:7], cl[3:7], gdn_conv, gdn_a_log, gdn_dt_bias, gdn_norm, with_ctx)
    filt_l = hyena_filter_bank(p_l.shape[1], hw1, hb1, hw2, hb2, hw3, hfreq)
    y_l = hyena_mixer(cl[7], hyena_conv, filt_l, hbias)
    mix_l = jnp.concatenate([a_l, g_l, y_l], axis=-1)
    mix_c = None
    if with_ctx:
        filt_c = hyena_filter_bank(p_c.shape[1], hw1, hb1, hw2, hb2, hw3, hfreq)
        y_c = hyena_mixer(cc[7], hyena_conv, filt_c, hbias)
        mix_c = jnp.concatenate([a_c, g_c, y_c], axis=-1)
    return mix_c, mix_l


def conv_ffn(u, w_up, conv_w, w_down):
    gate, val = jnp.split(dw_conv(u @ w_up, conv_w), 2, axis=-1)
    return (jax.nn.silu(gate) * val) @ w_down


def adaln(cvec, w_ada, b_ada):
    mod = jax.nn.silu(cvec) @ w_ada + b_ada
    return [m[:, None, :] for m in jnp.split(mod, 6, axis=-1)]


def setup_inputs(seed: int = 0) -> dict:
    key = jax.random.key(seed)
    ks = jax.random.split(key, 28)
    f32 = jnp.float32

    def nrm(k, shape, scale):
        return jax.random.normal(k, shape, f32) * scale

    def gain(k, shape):
        return 1.0 + 0.01 * jax.random.normal(k, shape, f32)
    dt = jnp.exp(jax.random.uniform(ks[14], (DEPTH, 2, GDN_HEADS), f32, math.log(1e-3), math.log(1e-1)))
    return {
        'x': nrm(ks[0], (BATCH, SEQ, D_MODEL), 1.0),
        'c': nrm(ks[1], (BATCH, D_MODEL), 1.0),
        'ctx': nrm(ks[2], (BATCH, CTX_LEN, D_MODEL), 1.0),
        'c_ctx': nrm(ks[3], (D_MODEL,), 1.0),
        'w_ada': nrm(ks[4], (DEPTH, D_MODEL, 6 * D_MODEL), D_MODEL ** -0.5),
        'b_ada': nrm(ks[5], (DEPTH, 6 * D_MODEL), 0.01),
        'norm1': gain(ks[6], (DEPTH, D_MODEL)),
        'norm2': gain(ks[7], (DEPTH, D_MODEL)),
        'w_in': nrm(ks[8], (DEPTH, D_MODEL, IN_WIDTH), D_MODEL ** -0.5),
        'w_out': nrm(ks[9], (DEPTH, MIX_WIDTH, D_MODEL), MIX_WIDTH ** -0.5),
        'q_gain': gain(ks[10], (DEPTH, HEAD_DIM)),
        'k_gain': gain(ks[11], (DEPTH, HEAD_DIM)),
        'gdn_conv': nrm(ks[12], (DEPTH, SHORT_CONV, GDN_QKV_WIDTH), SHORT_CONV ** -0.5),
        'gdn_a_log': jnp.log(jax.random.uniform(ks[13], (DEPTH, 2, GDN_HEADS), f32, 1.0, 16.0)),
        'gdn_dt_bias': dt + jnp.log(-jnp.expm1(-dt)),
        'gdn_norm': gain(ks[15], (DEPTH, GDN_DV)),
        'hyena_conv': nrm(ks[16], (DEPTH, HYENA_CONV, (HYENA_ORDER + 1) * HYENA_WIDTH), HYENA_CONV ** -0.5),
        'hyena_w1': nrm(ks[17], (DEPTH, HYENA_EMB_DIM, HYENA_FILTER_HIDDEN), HYENA_EMB_DIM ** -0.5),
        'hyena_b1': nrm(ks[18], (DEPTH, HYENA_FILTER_HIDDEN), 0.02),
        'hyena_w2': nrm(ks[19], (DEPTH, HYENA_FILTER_HIDDEN, HYENA_FILTER_HIDDEN), HYENA_FILTER_HIDDEN ** -0.5),
        'hyena_b2': nrm(ks[20], (DEPTH, HYENA_FILTER_HIDDEN), 0.02),
        'hyena_w3': nrm(ks[21], (DEPTH, HYENA_FILTER_HIDDEN, (HYENA_ORDER - 1) * 2 * HYENA_WIDTH),
                        0.1 * HYENA_FILTER_HIDDEN ** -0.5),
        'hyena_freq': gain(ks[22], (DEPTH, HYENA_FILTER_HIDDEN)),
        'hyena_bias': nrm(ks[23], (DEPTH, HYENA_ORDER - 1, HYENA_WIDTH), 1.0),
        'ffn_up': nrm(ks[24], (DEPTH, D_MODEL, 2 * D_FF), D_MODEL ** -0.5),
        'ffn_conv': nrm(ks[25], (DEPTH, FFN_CONV, 2 * D_FF), FFN_CONV ** -0.5),
        'ffn_down': nrm(ks[26], (DEPTH, D_FF, D_MODEL), D_FF ** -0.5),
        'final_norm': gain(ks[27], (D_MODEL,)),
    }


def reference(x, c, ctx, c_ctx, w_ada, b_ada, norm1, norm2, w_in, w_out, q_gain, k_gain,
              gdn_conv, gdn_a_log, gdn_dt_bias, gdn_norm, hyena_conv, hyena_w1, hyena_b1, hyena_w2,
              hyena_b2, hyena_w3, hyena_freq, hyena_bias, ffn_up, ffn_conv, ffn_down, final_norm):
    h, hc = x, ctx
    for i in range(DEPTH):
        with_ctx = i < DEPTH - 1
        sh1, sc1, g1, sh2, sc2, g2 = adaln(c, w_ada[i], b_ada[i])
        csh1, csc1, cg1, csh2, csc2, cg2 = adaln(c_ctx[None, :], w_ada[i], b_ada[i])
        u_l = rms_norm(h, norm1[i]) * (1 + sc1) + sh1
        u_c = rms_norm(hc, norm1[i]) * (1 + csc1) + csh1
        mix_c, mix_l = token_mixers(u_c @ w_in[i], u_l @ w_in[i], q_gain[i], k_gain[i],
                                    gdn_conv[i], gdn_a_log[i], gdn_dt_bias[i], gdn_norm[i],
                                    hyena_conv[i], hyena_w1[i], hyena_b1[i], hyena_w2[i], hyena_b2[i],
                                    hyena_w3[i], hyena_freq[i], hyena_bias[i], with_ctx)
        h = h + g1 * (mix_l @ w_out[i])
        h = h + g2 * conv_ffn(rms_norm(h, norm2[i]) * (1 + sc2) + sh2, ffn_up[i], ffn_conv[i], ffn_down[i])
        if with_ctx:
            hc = hc + cg1 * (mix_c @ w_out[i])
            hc = hc + cg2 * conv_ffn(rms_norm(hc, norm2[i]) * (1 + csc2) + csh2, ffn_up[i], ffn_conv[i], ffn_down[i])
    return rms_norm(h, final_norm)
```

```python
import numpy as np
import ml_dtypes
from contextlib import ExitStack
import concourse.bass as bass
import concourse.mybir as mybir
from concourse.bass_utils import run_bass_kernel_spmd

F32 = mybir.dt.float32
BF16 = mybir.dt.bfloat16
I32 = mybir.dt.int32
AF = mybir.ActivationFunctionType
ALU = mybir.AluOpType

ENGS = ("pe", "act", "dve", "pool", "sp")

D = 1024
NCH = 8
CTX = 256
LAT = 2048
T = CTX + LAT
NL = 4
DFF = 2816
NJ = DFF // 128
INW = 2576
EPS = 1e-6
TBS = [(0, 256), (256, 512), (768, 512), (1280, 512), (1792, 512)]
ARENA_WORDS = 20800


class DSem:
    def __init__(self, name):
        self.name = name
        self.total = 0
        self.h = None


class Prog:
    def __init__(self, nc):
        self.nc = nc
        self.ops = {e: [] for e in ENGS}
        self.cnt = {e: 0 for e in ENGS}
        self.waited = {e: {} for e in ENGS}
        self.lastw = {}
        self.readers = {}
        self.dsems = {}

    def dsem(self, name):
        if name not in self.dsems:
            self.dsems[name] = DSem(name)
        return self.dsems[name]

    def _need(self, eng, tok, waits):
        if tok is None:
            return
        kind, key, val = tok
        if kind == "e" and key == "pe" and eng == "pe":
            return
        if kind == "d":
            val = key.total
        cur = self.waited[eng].get(key, 0)
        if val > cur:
            self.waited[eng][key] = val
            waits[key] = (kind, key, val)

    def op(self, eng, fn, reads=(), writes=(), dsem=None):
        waits = {}
        for k in reads:
            self._need(eng, self.lastw.get(k), waits)
        for k in writes:
            self._need(eng, self.lastw.get(k), waits)
            for t in self.readers.get(k, ()):
                self._need(eng, t, waits)
        if dsem is not None:
            dsem.total += 16
            tok = ("d", dsem, dsem.total)
        else:
            self.cnt[eng] += 1
            tok = ("e", eng, self.cnt[eng])
        for k in reads:
            self.readers.setdefault(k, []).append(tok)
        for k in writes:
            self.lastw[k] = tok
            self.readers[k] = []
        self.ops[eng].append((list(waits.values()), fn, dsem))
        return tok

    def barrier(self):
        for eng in ENGS:
            waits = {}
            for e2 in ENGS:
                if self.cnt[e2] > 0:
                    self._need(eng, ("e", e2, self.cnt[e2]), waits)
            for d in self.dsems.values():
                if d.total > 0:
                    self._need(eng, ("d", d, d.total), waits)
            if waits:
                self.ops[eng].append((list(waits.values()), None, None))
        self.lastw = {}
        self.readers = {}

    def dma(self, eng, out, in_, dsem, reads=(), writes=()):
        return self.op(eng, lambda e: e.dma_start(out=out, in_=in_), reads, writes, dsem=dsem)

    def mm(self, out, lhsT, rhs, start, stop, reads=(), writes=()):
        return self.op("pe", lambda e: e.matmul(out, lhsT=lhsT, rhs=rhs, start=start, stop=stop), reads, writes)

    def transpose(self, out, in_, ident, reads=(), writes=()):
        return self.op("pe", lambda e: e.transpose(out=out, in_=in_, identity=ident), reads, writes)

    def act(self, out, in_, func, reads=(), writes=(), bias=None, scale=None):
        kw = {}
        if bias is not None:
            kw["bias"] = bias
        if scale is not None:
            kw["scale"] = scale
        return self.op("act", lambda e: e.activation(out=out, in_=in_, func=func, **kw), reads, writes)

    def tt(self, eng, out, in0, in1, op, reads=(), writes=()):
        return self.op(eng, lambda e: e.tensor_tensor(out=out, in0=in0, in1=in1, op=op), reads, writes)

    def ts(self, eng, out, in0, s1, s2, op0, op1=None, reads=(), writes=()):
        if op1 is None:
            return self.op(eng, lambda e: e.tensor_scalar(out=out, in0=in0, scalar1=s1, scalar2=None, op0=op0), reads, writes)
        return self.op(eng, lambda e: e.tensor_scalar(out=out, in0=in0, scalar1=s1, scalar2=s2, op0=op0, op1=op1), reads, writes)

    def stt(self, out, in0, scalar, in1, op0, op1, reads=(), writes=()):
        return self.op("dve", lambda e: e.scalar_tensor_tensor(out=out, in0=in0, scalar=scalar, in1=in1, op0=op0, op1=op1), reads, writes)

    def copy(self, eng, out, in_, reads=(), writes=()):
        if eng == "act":
            return self.op("act", lambda e: e.copy(out=out, in_=in_), reads, writes)
        return self.op(eng, lambda e: e.tensor_copy(out=out, in_=in_), reads, writes)

    def memset(self, eng, ap, val, writes=()):
        return self.op(eng, lambda e: e.memset(ap, val), (), writes)

    def recip(self, out, in_, reads=(), writes=()):
        return self.op("dve", lambda e: e.reciprocal(out=out, in_=in_), reads, writes)

    def emit(self, final_waits=()):
        nc = self.nc
        with ExitStack() as st:
            esem = {e: st.enter_context(nc.semaphore("es_" + e)) for e in ENGS}
            for d in self.dsems.values():
                d.h = st.enter_context(nc.semaphore("ds_" + d.name))
            block = st.enter_context(nc.Block())

            def run(engname, e):
                for waits, fn, dsem in self.ops[engname]:
                    for kind, key, val in waits:
                        e.wait_ge(esem[key] if kind == "e" else key.h, val)
                    if fn is None:
                        continue
                    ins = fn(e)
                    if dsem is not None:
                        ins.then_inc(dsem.h, 16)
                    else:
                        ins.then_inc(esem[engname], 1)
                if engname == "sp":
                    for d in final_waits:
                        e.wait_ge(d.h, d.total)

            @block.tensor
            def _(e):
                run("pe", e)

            @block.scalar
            def _(e):
                run("act", e)

            @block.vector
            def _(e):
                run("dve", e)

            @block.gpsimd
            def _(e):
                run("pool", e)

            @block.sync
            def _(e):
                run("sp", e)


class Packer:
    def __init__(self):
        self.cols = {}
        self.parts = []
        self.n = 0

    def add(self, name, arr):
        arr = np.ascontiguousarray(arr, dtype=np.float32).reshape(128, -1)
        self.cols[name] = (self.n, arr.shape[1])
        self.parts.append(arr)
        self.n += arr.shape[1]

    def pack(self):
        return np.concatenate(self.parts, axis=1)


def fm(v):
    v = np.asarray(v, dtype=np.float32)
    lead = v.shape[:-1]
    n = v.shape[-1] // 128
    v = v.reshape(lead + (n, 128))
    return np.moveaxis(v, -1, 0)


def small_layout(inputs, cvecs):
    pk = Packer()
    cT = np.zeros((128, 8, 4), np.float32)
    cT[:, :, 0:3] = np.moveaxis(fm(cvecs), 1, 2)
    pk.add("cT", cT)
    b = fm(inputs["b_ada"])
    pk.add("b_ada", np.repeat(b[:, :, :, None], 4, axis=3))
    pk.add("norm1", fm(inputs["norm1"]))
    pk.add("norm2", fm(inputs["norm2"]))
    pk.add("final_norm", fm(inputs["final_norm"]))
    pk.add("qg", np.tile(np.asarray(inputs["q_gain"], np.float32).T, (2, 1)))
    pk.add("kg", np.tile(np.asarray(inputs["k_gain"], np.float32).T, (2, 1)))
    pk.add("ffn_conv", fm(inputs["ffn_conv"]))
    pk.add("gdn_conv", fm(inputs["gdn_conv"]))
    rep = lambda a: np.broadcast_to(np.asarray(a, np.float32).reshape(1, NL, -1), (128, NL, np.asarray(a).reshape(NL, -1).shape[1]))
    pk.add("alog_bc", rep(inputs["gdn_a_log"]))
    pk.add("dtb_bc", rep(inputs["gdn_dt_bias"]))
    pk.add("gn_bc", rep(inputs["gdn_norm"]))
    pk.add("hy_conv", fm(inputs["hyena_conv"]))
    pk.add("hy_bias", fm(np.asarray(inputs["hyena_bias"])[:, 0, :]))
    def rows(a, n):
        a = np.asarray(a, np.float32)
        o = np.zeros((128,) + (a.shape[0],) + a.shape[2:], np.float32)
        o[:n] = np.moveaxis(a, 1, 0)
        return o
    pk.add("hw1", rows(inputs["hyena_w1"], 33))
    pk.add("hw2", rows(inputs["hyena_w2"], 64))
    pk.add("hb1", rows(np.asarray(inputs["hyena_b1"])[:, :, None], 64))
    pk.add("hb2", rows(np.asarray(inputs["hyena_b2"])[:, :, None], 64))
    pk.add("hfr", rows(np.asarray(inputs["hyena_freq"])[:, :, None], 64))
    return pk


def rope_tables():
    inv = 10000.0 ** (-np.arange(16, dtype=np.float64) / 16.0)
    t = np.arange(LAT)
    row = (t // 64).astype(np.float64)
    col = (t % 64).astype(np.float64)
    cos = np.zeros((128, LAT), np.float64)
    sin = np.zeros((128, LAT), np.float64)
    for p in range(128):
        dd = p % 64
        pos = row if dd < 32 else col
        ang = pos * inv[dd % 16]
        cos[p] = np.cos(ang)
        sin[p] = np.sin(ang)
    R = np.zeros((128, 128), np.float32)
    for blk in range(4):
        for i in range(16):
            R[blk * 32 + i, blk * 32 + i + 16] = -1.0
            R[blk * 32 + i + 16, blk * 32 + i] = 1.0
    return cos.astype(np.float32), sin.astype(np.float32), np.ascontiguousarray(R.T)


def hyena_consts(L):
    f32 = np.float32
    n = L
    t = np.linspace(0.0, 1.0, n, dtype=f32)
    omega = (f32(2.0 * np.pi) * np.arange(n, dtype=f32) / f32(n)).astype(f32)
    bands = np.linspace(1e-4, 15, 16, dtype=f32)
    ang = (bands[None, :] * omega[:, None]).astype(f32)
    feats = np.concatenate([t[:, None], np.cos(ang), -np.sin(ang)], axis=1).astype(f32)
    ntc = L // 128
    negt = np.ascontiguousarray((-t).reshape(ntc, 128).T)
    nfc = L // 128
    tbw = min(512, L)
    ntb = L // tbw
    ff = np.arange(L, dtype=np.int64)
    tt = np.arange(L, dtype=np.int64)
    k = ((2 * ff[None, :] + 1) * tt[:, None]) % (4 * L)
    a = 2.0 * np.pi * k.astype(np.float64) / (4.0 * L)
    C = np.cos(a)
    S = np.sin(a)
    bf = ml_dtypes.bfloat16
    fwdC = C.reshape(ntc, 128, nfc, 128).transpose(2, 1, 0, 3).astype(bf)
    fwdS = S.reshape(ntc, 128, nfc, 128).transpose(2, 1, 0, 3).astype(bf)
    invC = (C.T / L).reshape(nfc, 128, ntb, tbw).transpose(2, 1, 0, 3).astype(bf)
    invS = (S.T / L).reshape(nfc, 128, ntb, tbw).transpose(2, 1, 0, 3).astype(bf)
    return dict(featsT=np.ascontiguousarray(feats.T), negt=negt,
                fwdC=np.ascontiguousarray(fwdC), fwdS=np.ascontiguousarray(fwdS),
                invC=np.ascontiguousarray(invC), invS=np.ascontiguousarray(invS))


def gdn_masks():
    idx = np.arange(128)
    out = []
    for d in range(2):
        if d == 0:
            eo = idx[:, None] <= idx[None, :]
            st = idx[:, None] < idx[None, :]
        else:
            eo = idx[:, None] >= idx[None, :]
            st = idx[:, None] > idx[None, :]
        tri = eo.astype(np.float32)
        mb = np.where(eo, 0.0, -30000.0).astype(np.float32)
        stf = st.astype(np.float32)
        out += [tri, -tri, mb, mb, stf, stf]
    same = lambda n: (idx[:, None] // n) == (idx[None, :] // n)
    bd16 = same(16).astype(np.float32)
    out += [bd16, bd16]
    for n in (32, 64, 128):
        off = (same(n) & ~same(n // 2)).astype(np.float32)
        out += [off, off]
    i64 = np.zeros((128, 128), np.float32)
    i64[:64, 0:64] = np.eye(64)
    i64[:64, 64:128] = np.eye(64)
    out += [i64, np.ones((128, 128), np.float32)]
    return np.ascontiguousarray(np.concatenate(out, axis=1))


def hyena_deltas():
    f32 = np.float32
    max_decay = np.log(1e-2) / 0.3
    min_decay = np.log(1e-2) / 1.5
    d = np.abs(np.linspace(min_decay, max_decay, 256, dtype=f32)).astype(f32)
    return np.ascontiguousarray(np.broadcast_to(d[None, :], (128, 256)))

def build(cfg, cols):
    NS = cfg.get("NS", 2)
    LAYERS = cfg.get("layers", NL)
    do_attn = cfg.get("attn", True)
    do_ffn = cfg.get("ffn", True)
    raw_out = cfg.get("raw_out", False)
    nsmall = cfg["nsmall"]

    nc = bass.Bass("TRN2", target_bir_lowering=False)
    P = Prog(nc)

    def dram(name, shape, dt=F32, kind="ExternalInput"):
        return nc.dram_tensor(name, shape, dt, kind=kind).ap()

    h0 = dram("h0", [NS, D, T])
    smallp_d = dram("smallp", [128, nsmall])
    w_ada = dram("w_ada", [NL, D, 6 * D])
    w_in = dram("w_in", [NL, D, INW])
    w_out = dram("w_out", [NL, D, D])
    ffn_up = dram("ffn_up", [NL, D, 2 * DFF])
    ffn_down = dram("ffn_down", [NL, DFF, D])
    ropecs_d = dram("ropecs", [2, 128, LAT])
    ropeR_d = dram("ropeR", [128, 128])
    ident_d = dram("ident", [128, 128])
    hyc = {}
    for L_ in (LAT, CTX):
        nb = L_ // 128
        tbw = min(512, L_)
        hyc[L_] = dict(featsT=dram("hy_featsT%d" % L_, [33, L_]), negt=dram("hy_negt%d" % L_, [128, nb]),
                       fwdC=dram("hy_fwdC%d" % L_, [nb, 128, nb, 128], BF16), fwdS=dram("hy_fwdS%d" % L_, [nb, 128, nb, 128], BF16),
                       invC=dram("hy_invC%d" % L_, [L_ // tbw, 128, nb, tbw], BF16), invS=dram("hy_invS%d" % L_, [L_ // tbw, 128, nb, tbw], BF16))
    hy_delta_d = dram("hy_delta", [128, 256])
    gmask_d = dram("gmask", [128, 2816])
    hw3_d = dram("hyena_w3", [NL, 64, 512])
    do_hyena = cfg.get("hyena", True)
    do_gdn = cfg.get("gdn", True)
    if raw_out:
        out_d = dram("out", [NS, D, T], kind="ExternalOutput")
    else:
        out_d = dram("out", [NS, D, LAT], kind="ExternalOutput")

    with ExitStack() as st:
        def sb(name, shape, dt=F32):
            return st.enter_context(nc.sbuf_tensor(name, shape, dt))

        hT = sb("hT", [128, NCH, T])
        uT = sb("uT", [128, NCH, T], BF16)
        smallp = sb("smallp_sb", [128, nsmall])
        modT = sb("modT", [128, NL, 48, 4])
        AB = sb("AB", [128, 4, 8])
        scT = sb("scT", [128, 8, 4])
        identf = sb("identf", [128, 128])
        identb = sb("identb", [128, 128], BF16)
        ones_mean = sb("ones_mean", [128, 128], BF16)
        bd_mean = sb("bd_mean", [128, 128], BF16)
        ones_col = sb("ones_col", [128, 64], BF16)
        ropeRf = sb("ropeRf", [128, 128])
        ropeRb = sb("ropeRb", [128, 128], BF16)
        hy_negt = {LAT: sb("hy_negtL", [128, 16]), CTX: sb("hy_negtC", [128, 2])}
        hy_delta = sb("hy_delta_sb", [128, 256])
        arena = sb("arena", [128, ARENA_WORDS])
        pbank = [st.enter_context(nc.psum_tensor("pb%d" % i, [128, 512], F32)) for i in range(8)]

        def sp(name):
            o, n = cols[name]
            return smallp[:, o:o + n]

        class Arena:
            def __init__(self):
                self.off = 0

            def reset(self):
                self.off = 0

            def at(self, off):
                self.off = off

            def take(self, shape, dt=F32):
                n = int(np.prod(shape[1:]))
                words = n if dt == F32 else (n + 1) // 2
                a = arena[:, self.off:self.off + words]
                self.off += words
                assert self.off <= ARENA_WORDS, ("arena overflow", self.off)
                if dt != F32:
                    a = a.bitcast(dt)
                    a = a[:, 0:n]
                if len(shape) == 3:
                    a = a.rearrange("p (a b) -> p a b", a=shape[1])
                elif len(shape) == 4:
                    a = a.rearrange("p (a b c) -> p a b c", a=shape[1], b=shape[2])
                return a

        AR = Arena()
        dbg_on = cfg.get("dbg", False)
        ddbg = P.dsem("dbg")

        def dbg_dump(name, ap, dt=F32):
            if not dbg_on or name in dbg_done:
                return
            dbg_done.add(name)
            P.barrier()
            shp = list(ap.shape)
            dd = nc.dram_tensor("dbg_" + name, shp, dt, kind="ExternalOutput").ap()
            P.dma("sp", dd, ap, ddbg)
            P.barrier()
        dbg_done = set()

        class Banks:
            def __init__(self, ids):
                self.ids = list(ids)
                self.i = 0

            def next(self):
                b = self.ids[self.i % len(self.ids)]
                self.i += 1
                return b

        class WStream:
            def __init__(self, nstg, nwb):
                self.stg = [AR.take([128, 8, 128]) for _ in range(nstg)]
                self.wb = [AR.take([128, 8, 128], BF16) for _ in range(nwb)]
                self.ds = [P.dsem("wstg%d" % i) for i in range(nstg)]
                self.i = 0
                self.j = 0

            def load(self, pieces, nk):
                si = self.i % len(self.stg)
                wi = self.j % len(self.wb)
                self.i += 1
                self.j += 1
                stg, wb = self.stg[si], self.wb[wi]
                for src, co, w in pieces:
                    P.dma("sp", stg[:, 0:nk, co:co + w], src, self.ds[si], writes=[("stg", si)])
                width = max(co + w for _, co, w in pieces)
                P.copy("pool", wb[:, 0:nk, 0:width], stg[:, 0:nk, 0:width], reads=[("stg", si)], writes=[("wb", wi)])
                return wb, ("wb", wi)

        def wview(w, l):
            return w[l].rearrange("(k p) n -> p k n", p=128)

        dc = P.dsem("const")
        P.dma("sp", smallp[:, :], smallp_d[:, :], dc, writes=["smallp"])
        P.dma("sp", identf[:, :], ident_d[:, :], dc, writes=["identf"])
        P.dma("sp", ropeRf[:, :], ropeR_d[:, :], dc, writes=["ropeRf"])
        P.dma("sp", hy_negt[LAT][:, :], hyc[LAT]["negt"][:, :], dc, writes=["hyc"])
        P.dma("sp", hy_negt[CTX][:, :], hyc[CTX]["negt"][:, :], dc, writes=["hyc"])
        P.dma("sp", hy_delta[:, :], hy_delta_d[:, :], dc, writes=["hyc"])
        P.copy("dve", identb[:, :], identf[:, :], reads=["identf"], writes=["identb"])
        P.copy("dve", ropeRb[:, :], ropeRf[:, :], reads=["ropeRf"], writes=["ropeRb"])
        P.memset("dve", ones_mean[:, :], 1.0 / D, writes=["ones_mean"])
        P.memset("dve", bd_mean[:, :], 0.0, writes=["bd_mean"])
        P.memset("dve", bd_mean[0:64, 0:64], 1.0 / 64, writes=["bd_mean"])
        P.memset("dve", bd_mean[64:128, 64:128], 1.0 / 64, writes=["bd_mean"])
        P.memset("dve", ones_col[:, :], 1.0, writes=["ones_col"])

        cT = sp("cT").rearrange("p (k j) -> p k j", k=8)
        P.act(scT[:, :, :], cT, AF.Silu, reads=["smallp"], writes=["scT"])
        AR.reset()
        NAST = 4
        astg = [AR.take([128, 8, 512]) for _ in range(NAST)]
        ads = [P.dsem("astg%d" % i_) for i_ in range(NAST)]
        b_ada = sp("b_ada").rearrange("p (l m j) -> p l m j", l=NL, m=48)
        n = 0
        for l in range(LAYERS):
            wv = wview(w_ada, l)
            pm = pbank[l % 2]
            for cb in range(12):
                si = n % NAST
                n += 1
                P.dma("sp" if n % 2 else "act", astg[si][:, :, :], wv[:, :, cb * 512:(cb + 1) * 512], ads[si], writes=[("astg", si)])
                for mm_ in range(4):
                    m = cb * 4 + mm_
                    for k in range(8):
                        P.mm(pm[:, m * 4:(m + 1) * 4], astg[si][:, k, mm_ * 128:(mm_ + 1) * 128], scT[:, k, :],
                             k == 0, k == 7, reads=[("astg", si), "scT"], writes=[("pmod", l % 2)])
            P.tt("dve", modT[:, l, :, :], pm[:, 0:192].rearrange("p (m j) -> p m j", m=48), b_ada[:, l, :, :], ALU.add,
                 reads=[("pmod", l % 2), "smallp"], writes=["modT"])
        P.barrier()

        norm1 = sp("norm1").rearrange("p (l k) -> p l k", l=NL)
        norm2 = sp("norm2").rearrange("p (l k) -> p l k", l=NL)
        fnorm = sp("final_norm")
        qg = sp("qg")
        kg = sp("kg")
        ffn_conv = sp("ffn_conv").rearrange("p (l j m) -> p l j m", l=NL, j=3)

        def mod_ap(l, which, j):
            return modT[:, l, which * 8:(which + 1) * 8, j]

        def tb_j(tb, s):
            return 2 if tb == 0 else s

        def norm_modulate(l, s, normw, sc_idx, sh_idx, tbs, banks):
            for idx, j in ((0, s), (1, 2)):
                P.stt(AB[:, idx, :], mod_ap(l, sc_idx, j), 1.0, normw[:, l, :], ALU.add, ALU.mult,
                      reads=["modT", "smallp"], writes=["AB"])
            sq = [AR.take([128, 512], BF16) for _ in range(2)]
            rs = [AR.take([128, 512]) for _ in range(2)]
            tmp = [AR.take([128, 512]) for _ in range(2)]
            cnt = 0
            for tb in tbs:
                t0, tl = TBS[tb]
                pbk = banks.next()
                ps = pbank[pbk]
                for k in range(NCH):
                    q = sq[cnt % 2]
                    cnt += 1
                    P.act(q[:, 0:tl], hT[:, k, t0:t0 + tl], AF.Square, reads=[("hT", k, tb)], writes=[("sq", id(q))])
                    P.mm(ps[:, 0:tl], ones_mean[:, :], q[:, 0:tl], k == 0, k == NCH - 1,
                         reads=[("sq", id(q)), "ones_mean"], writes=[("pb", pbk)])
                r = rs[tb % 2]
                P.act(r[:, 0:tl], ps[:, 0:tl], AF.Sqrt, bias=EPS, scale=1.0, reads=[("pb", pbk)], writes=[("rs", tb % 2)])
                P.recip(r[:, 0:tl], r[:, 0:tl], reads=[("rs", tb % 2)], writes=[("rs", tb % 2)])
                ai = 1 if tb == 0 else 0
                jj = tb_j(tb, s)
                for k in range(NCH):
                    tm = tmp[k % 2]
                    P.tt("pool" if k % 2 else "dve", tm[:, 0:tl], hT[:, k, t0:t0 + tl], r[:, 0:tl], ALU.mult,
                         reads=[("hT", k, tb), ("rs", tb % 2)], writes=[("ntmp", k % 2)])
                    P.act(uT[:, k, t0:t0 + tl], tm[:, 0:tl], AF.Identity, scale=AB[:, ai, k:k + 1],
                          bias=mod_ap(l, sh_idx, jj)[:, k:k + 1],
                          reads=[("ntmp", k % 2), "AB", "modT"], writes=[("uT", tb)])

        def wout_accumulate(ws, l, s, mixchunk, src, skey, tbs, banks):
            wt, wk = ws.load([(w_out[l, mixchunk * 128:(mixchunk + 1) * 128, :].rearrange("p (k n) -> p k n", k=8), 0, 128)], 8)
            for m in range(NCH):
                for tb in tbs:
                    t0, tl = TBS[tb]
                    b1 = banks.next()
                    P.mm(pbank[b1][:, 0:tl], wt[:, m, :], src[:, t0:t0 + tl], True, True,
                         reads=[wk, skey], writes=[("pb", b1)])
                    P.stt(hT[:, m, t0:t0 + tl], pbank[b1][:, 0:tl], mod_ap(l, 2, tb_j(tb, s))[:, m:m + 1],
                          hT[:, m, t0:t0 + tl], ALU.mult, ALU.add,
                          reads=[("pb", b1), "modT", ("hT", m, tb)], writes=[("hT", m, tb)])

        dio = P.dsem("io")
        dout = P.dsem("out")
        for s in range(NS):
            P.barrier()
            for k in range(NCH):
                P.dma("sp", hT[:, k, :], h0[s, k * 128:(k + 1) * 128, :], dio, writes=[("hT", k, tb) for tb in range(5)])
            for l in range(LAYERS):
                last = (l == NL - 1)
                tbs_all = list(range(5))
                tbs_out = [1, 2, 3, 4] if last else tbs_all
                wv_in = wview(w_in, l)
                wv_out = wview(w_out, l)

                P.barrier()
                AR.reset()
                norm_modulate(l, s, norm1, 1, 0, tbs_all, Banks([6, 7]))

                if do_attn:
                    P.barrier()
                    AR.reset()
                    ws = WStream(3, 4)
                    ropec = AR.take([128, LAT], BF16)
                    ropes = AR.take([128, LAT], BF16)
                    rstage = AR.take([128, LAT])
                    dr = P.dsem("rope")
                    for i, dst in enumerate((ropec, ropes)):
                        P.dma("sp", rstage[:, :], ropecs_d[i], dr, writes=["rstage"])
                        P.copy("pool", dst[:, :], rstage[:, :], reads=["rstage"], writes=[("rope", i)])
                    P.barrier()
                    AR.off -= LAT
                    kdup = AR.take([128, 2, T], BF16)
                    vtok = AR.take([128, 18, 128], BF16)
                    qT = AR.take([128, T], BF16)
                    aT = AR.take([128, T], BF16)
                    sqb = [AR.take([128, 512], BF16) for _ in range(2)]
                    rsb = [AR.take([128, 512]) for _ in range(2)]
                    xnb = [AR.take([128, 512], BF16) for _ in range(2)]
                    t1b = [AR.take([128, 512]) for _ in range(2)]
                    pT = [AR.take([128, 512], BF16) for _ in range(4)]
                    rden = AR.take([128, 512])
                    pjb = Banks([6, 7])
                    cntr = [0]

                    def qk_project(wtile, wkey, gain, dst_of_tb, dkey):
                        for tb in tbs_all:
                            t0, tl = TBS[tb]
                            i2 = cntr[0] % 2
                            cntr[0] += 1
                            b1 = pjb.next()
                            ps = pbank[b1]
                            for k in range(NCH):
                                P.mm(ps[:, 0:tl], wtile[:, k, :], uT[:, k, t0:t0 + tl], k == 0, k == NCH - 1,
                                     reads=[wkey, ("uT", tb)], writes=[("pb", b1)])
                            P.act(sqb[i2][:, 0:tl], ps[:, 0:tl], AF.Square, reads=[("pb", b1)], writes=[("sqb", i2)])
                            b2 = pjb.next()
                            ps2 = pbank[b2]
                            P.mm(ps2[:, 0:tl], bd_mean[:, :], sqb[i2][:, 0:tl], True, True,
                                 reads=[("sqb", i2), "bd_mean"], writes=[("pb", b2)])
                            P.act(rsb[i2][:, 0:tl], ps2[:, 0:tl], AF.Sqrt, bias=EPS, scale=1.0,
                                  reads=[("pb", b2)], writes=[("rsb", i2)])
                            P.recip(rsb[i2][:, 0:tl], rsb[i2][:, 0:tl], reads=[("rsb", i2)], writes=[("rsb", i2)])
                            dst = dst_of_tb(t0, tl)
                            if tb == 0:
                                P.stt(dst, ps[:, 0:tl], gain, rsb[i2][:, 0:tl], ALU.mult, ALU.mult,
                                      reads=[("pb", b1), ("rsb", i2), "smallp"], writes=[dkey(tb)])
                                continue
                            P.stt(xnb[i2][:, 0:tl], ps[:, 0:tl], gain, rsb[i2][:, 0:tl], ALU.mult, ALU.mult,
                                  reads=[("pb", b1), ("rsb", i2), "smallp"], writes=[("xnb", i2)])
                            P.mm(ps2[:, 0:tl], ropeRb[:, :], xnb[i2][:, 0:tl], True, True,
                                 reads=[("xnb", i2), "ropeRb"], writes=[("pb", b2)])
                            l0 = t0 - CTX
                            P.tt("pool", t1b[i2][:, 0:tl], xnb[i2][:, 0:tl], ropec[:, l0:l0 + tl], ALU.mult,
                                 reads=[("xnb", i2), ("rope", 0)], writes=[("t1b", i2)])
                            P.tt("dve", xnb[i2][:, 0:tl], ps2[:, 0:tl], ropes[:, l0:l0 + tl], ALU.mult,
                                 reads=[("pb", b2), ("rope", 1)], writes=[("xnb", i2)])
                            P.tt("pool", dst, t1b[i2][:, 0:tl], xnb[i2][:, 0:tl], ALU.add,
                                 reads=[("t1b", i2), ("xnb", i2)], writes=[dkey(tb)])

                    for g in range(2):
                        c0 = 512 + 64 * g
                        wt, wk = ws.load([(wv_in[:, :, c0:c0 + 64], 0, 64), (wv_in[:, :, c0:c0 + 64], 64, 64)], 8)
                        qk_project(wt, wk, kg[:, l:l + 1], lambda t0, tl, g=g: kdup[:, g, t0:t0 + tl], lambda tb, g=g: ("kdup", g, tb))
                    wt, wk = ws.load([(wv_in[:, :, 640:768], 0, 128)], 8)
                    for i in range(18):
                        b1 = pjb.next()
                        ps = pbank[b1]
                        tb = 0 if i < 2 else 1 + (i - 2) // 4
                        for k in range(NCH):
                            P.mm(ps[:, 0:128], uT[:, k, i * 128:(i + 1) * 128], wt[:, k, :], k == 0, k == NCH - 1,
                                 reads=[wk, ("uT", tb)], writes=[("pb", b1)])
                        P.copy("act", vtok[:, i, :], ps[:, 0:128], reads=[("pb", b1)], writes=[("vtok", i)])

                    sbanks = [Banks([0, 1]), Banks([2, 3])]
                    for c in range(4):
                        g = c // 2
                        wt, wk = ws.load([(wv_in[:, :, c * 128:(c + 1) * 128], 0, 128)], 8)
                        qk_project(wt, wk, qg[:, l:l + 1], lambda t0, tl: qT[:, t0:t0 + tl], lambda tb: ("qT", tb))
                        for qb in tbs_out:
                            q0, ql = TBS[qb]
                            kts = [0, 1] if qb == 0 else list(range(18))
                            pend = None
                            for ii, kt in enumerate(kts):
                                ktb = 0 if kt < 2 else 1 + (kt - 2) // 4
                                cur = []
                                for hh in range(2):
                                    bk = sbanks[hh].next()
                                    pr = slice(64 * hh, 64 * hh + 64)
                                    P.mm(pbank[bk][:, 0:ql], kdup[pr, g, kt * 128:(kt + 1) * 128], qT[pr, q0:q0 + ql], True, True,
                                         reads=[("kdup", g, ktb), ("qT", qb)], writes=[("pb", bk)])
                                    cur.append(bk)
                                if pend is not None:
                                    pend()
                                pts = []
                                for hh in range(2):
                                    pi = (2 * ii + hh) % 4
                                    P.act(pT[pi][:, 0:ql], pbank[cur[hh]][:, 0:ql], AF.Exp, scale=0.125,
                                          reads=[("pb", cur[hh])], writes=[("pT", pi)])
                                    pts.append(pi)

                                def pv(kt=kt, pts=pts, first=(ii == 0), lastk=(ii == len(kts) - 1), ql=ql, g=g):
                                    for hh in range(2):
                                        pr = slice(64 * hh, 64 * hh + 64)
                                        P.mm(pbank[4][pr, 0:ql], vtok[:, kt, 64 * g:64 * g + 64], pT[pts[hh]][:, 0:ql], first, lastk,
                                             reads=[("pT", pts[hh]), ("vtok", kt)], writes=[("pb", 4)])
                                        P.mm(pbank[5][pr, 0:ql], ones_col[:, :], pT[pts[hh]][:, 0:ql], first, lastk,
                                             reads=[("pT", pts[hh]), "ones_col"], writes=[("pb", 5)])
                                pend = pv
                            pend()
                            P.recip(rden[:, 0:ql], pbank[5][:, 0:ql], reads=[("pb", 5)], writes=["rden"])
                            P.tt("dve", aT[:, q0:q0 + ql], pbank[4][:, 0:ql], rden[:, 0:ql], ALU.mult,
                                 reads=[("pb", 4), "rden"], writes=[("aT", qb)])
                        for m in range(NCH):
                            pass
                        wt, wk = ws.load([(w_out[l, c * 128:(c + 1) * 128, :].rearrange("p (k n) -> p k n", k=8), 0, 128)], 8)
                        for m in range(NCH):
                            for tb in tbs_out:
                                t0, tl = TBS[tb]
                                b1 = pjb.next()
                                P.mm(pbank[b1][:, 0:tl], wt[:, m, :], aT[:, t0:t0 + tl], True, True,
                                     reads=[wk, ("aT", tb)], writes=[("pb", b1)])
                                P.stt(hT[:, m, t0:t0 + tl], pbank[b1][:, 0:tl], mod_ap(l, 2, tb_j(tb, s))[:, m:m + 1],
                                      hT[:, m, t0:t0 + tl], ALU.mult, ALU.add,
                                      reads=[("pb", b1), "modT", ("hT", m, tb)], writes=[("hT", m, tb)])


                if do_gdn:
                    P.barrier()
                    AR.reset()
                    gdn_conv = sp("gdn_conv").rearrange("p (l j m) -> p l j m", l=NL, j=3)
                    alog_bc = sp("alog_bc").rearrange("p (l w) -> p l w", l=NL)
                    dtb_bc = sp("dtb_bc").rearrange("p (l w) -> p l w", l=NL)
                    gn_bc = sp("gn_bc").rearrange("p (l w) -> p l w", l=NL)
                    segs = [(0, CTX), (CTX, T)]
                    NT = 18
                    beta = AR.take([128, NT, 8])
                    gdec = AR.take([128, NT, 8])
                    gm = AR.take([128, 2816])
                    G0 = AR.off
                    P.dma("sp", gm[:, :], gmask_d[:, :], P.dsem("gmask"), writes=["gm"])
                    TriI = [gm[:, d_ * 768:d_ * 768 + 128] for d_ in range(2)]
                    NegTri = [gm[:, d_ * 768 + 128:d_ * 768 + 256] for d_ in range(2)]
                    MaskB = [gm[:, d_ * 768 + 256:d_ * 768 + 512] for d_ in range(2)]
                    Strict = [gm[:, d_ * 768 + 512:d_ * 768 + 768] for d_ in range(2)]
                    v3 = lambda ap_: ap_.rearrange("p (a b) -> p a b", a=2)
                    BD16 = v3(gm[:, 1536:1792])
                    OFF = {32: v3(gm[:, 1792:2048]), 64: v3(gm[:, 2048:2304]), 128: v3(gm[:, 2304:2560])}
                    I64 = gm[:, 2560:2688]
                    ones128 = gm[:, 2688:2816]
                    ws = WStream(2, 3)
                    bdraw = AR.take([128, NT, 16])
                    tmpa = AR.take([128, NT, 8])
                    tmpb = AR.take([128, NT, 8])
                    expA = AR.take([128, 8])
                    wt, wk = ws.load([(wv_in[:, :, 1792:1808], 0, 16)], 8)
                    gb8 = Banks(list(range(8)))
                    for i in range(NT):
                        b1 = gb8.next()
                        tb = 0 if i < 2 else 1 + (i - 2) // 4
                        for k in range(NCH):
                            P.mm(pbank[b1][:, 0:16], uT[:, k, i * 128:(i + 1) * 128], wt[:, k, 0:16], k == 0, k == NCH - 1,
                                 reads=[wk, ("uT", tb)], writes=[("pb", b1)])
                        P.copy("act", bdraw[:, i, :], pbank[b1][:, 0:16], reads=[("pb", b1)], writes=["bdraw"])
                    P.act(beta[:, :, :], bdraw[:, :, 0:8], AF.Sigmoid, reads=["bdraw"], writes=["beta"])
                    P.tt("dve", tmpa[:, :, :], bdraw[:, :, 8:16], dtb_bc[:, l, :].unsqueeze(1).broadcast_to([128, NT, 8]), ALU.add,
                         reads=["bdraw", "smallp"], writes=["tmpa"])
                    P.act(tmpb[:, :, :], tmpa[:, :, :], AF.Abs, reads=["tmpa"], writes=["tmpb"])
                    P.act(tmpb[:, :, :], tmpb[:, :, :], AF.Exp, scale=-1.0, reads=["tmpb"], writes=["tmpb"])
                    P.act(tmpb[:, :, :], tmpb[:, :, :], AF.Ln, bias=1.0, scale=1.0, reads=["tmpb"], writes=["tmpb"])
                    P.ts("dve", tmpa[:, :, :], tmpa[:, :, :], 0.0, None, ALU.max, reads=["tmpa"], writes=["tmpa"])
                    P.tt("pool", tmpa[:, :, :], tmpa[:, :, :], tmpb[:, :, :], ALU.add, reads=["tmpa", "tmpb"], writes=["tmpa"])
                    P.act(expA[:, :], alog_bc[:, l, :], AF.Exp, reads=["smallp"], writes=["expA"])
                    P.stt(gdec[:, :, :], tmpa[:, :, :], -1.0, expA[:, :].unsqueeze(1).broadcast_to([128, NT, 8]), ALU.mult, ALU.mult,
                          reads=["tmpa", "expA"], writes=["gdec"])

                    gstop = cfg.get("gstop", 0)
                    for pair in (range(2) if gstop != 1 else []):
                        P.barrier()
                        AR.at(G0)
                        qT_ = AR.take([128, T], BF16)
                        kT_ = AR.take([128, T], BF16)
                        vT_ = AR.take([128, T], BF16)
                        Oacc = AR.take([128, NT, 128])
                        G1 = AR.off
                        ws = WStream(2, 3)
                        raw = AR.take([128, T], BF16)
                        acc = AR.take([128, T])
                        sqb = [AR.take([128, 512], BF16) for _ in range(2)]
                        rsb = [AR.take([128, 512]) for _ in range(2)]
                        pjb = Banks([6, 7])
                        sbk = Banks([4, 5])
                        for which, cbase, dst in (("q", 768, qT_), ("k", 1024, kT_), ("v", 1280, vT_)):
                            col0 = cbase + pair * 128
                            mcol = {"q": 0, "k": 2, "v": 4, "gate": 0}[which] + pair
                            wt, wk = ws.load([(wv_in[:, :, col0:col0 + 128], 0, 128)], 8)
                            for tb in tbs_all:
                                t0, tl = TBS[tb]
                                b1 = pjb.next()
                                for k in range(NCH):
                                    P.mm(pbank[b1][:, 0:tl], wt[:, k, :], uT[:, k, t0:t0 + tl], k == 0, k == NCH - 1,
                                         reads=[wk, ("uT", tb)], writes=[("pb", b1)])
                                if which == "gate":
                                    P.act(gsil[:, t0:t0 + tl], pbank[b1][:, 0:tl], AF.Silu, reads=[("pb", b1)], writes=["gsil"])
                                else:
                                    P.copy("act", raw[:, t0:t0 + tl], pbank[b1][:, 0:tl], reads=[("pb", b1)], writes=["graw"])
                            if which == "gate":
                                continue
                            for (a, b) in segs:
                                P.act(acc[:, a:b], raw[:, a:b], AF.Identity, scale=gdn_conv[:, l, 1, mcol:mcol + 1],
                                      reads=["graw", "smallp"], writes=["gacc"])
                                P.stt(acc[:, a + 1:b], raw[:, a:b - 1], gdn_conv[:, l, 0, mcol:mcol + 1], acc[:, a + 1:b], ALU.mult, ALU.add,
                                      reads=["graw", "gacc", "smallp"], writes=["gacc"])
                                P.stt(acc[:, a:b - 1], raw[:, a + 1:b], gdn_conv[:, l, 2, mcol:mcol + 1], acc[:, a:b - 1], ALU.mult, ALU.add,
                                      reads=["graw", "gacc", "smallp"], writes=["gacc"])
                            if which == "v":
                                P.act(vT_[:, :], acc[:, :], AF.Silu, reads=["gacc"], writes=["vT_"])
                                continue
                            P.act(acc[:, :], acc[:, :], AF.Silu, reads=["gacc"], writes=["gacc"])
                            for tb in tbs_all:
                                t0, tl = TBS[tb]
                                i2 = tb % 2
                                P.act(sqb[i2][:, 0:tl], acc[:, t0:t0 + tl], AF.Square, reads=["gacc"], writes=[("gsq", i2)])
                                b2 = sbk.next()
                                P.mm(pbank[b2][:, 0:tl], bd_mean[:, :], sqb[i2][:, 0:tl], True, True,
                                     reads=[("gsq", i2), "bd_mean"], writes=[("pb", b2)])
                                P.act(rsb[i2][:, 0:tl], pbank[b2][:, 0:tl], AF.Sqrt, bias=EPS, scale=64.0,
                                      reads=[("pb", b2)], writes=[("grs", i2)])
                                P.recip(rsb[i2][:, 0:tl], rsb[i2][:, 0:tl], reads=[("grs", i2)], writes=[("grs", i2)])
                                if which == "q":
                                    P.stt(dst[:, t0:t0 + tl], acc[:, t0:t0 + tl], 0.125, rsb[i2][:, 0:tl], ALU.mult, ALU.mult,
                                          reads=["gacc", ("grs", i2)], writes=["qT_"])
                                else:
                                    P.tt("dve", dst[:, t0:t0 + tl], acc[:, t0:t0 + tl], rsb[i2][:, 0:tl], ALU.mult,
                                         reads=["gacc", ("grs", i2)], writes=["kT_"])
                        if pair == 0:
                            dbg_dump("g_beta", beta[:, :, :]); dbg_dump("g_gdec", gdec[:, :, :])
                            dbg_dump("g_qT", qT_[:, :], BF16); dbg_dump("g_kT", kT_[:, :], BF16); dbg_dump("g_vT", vT_[:, :], BF16)
                        if gstop == 2:
                            continue
                        P.barrier()
                        AR.at(G1)

                        def make_set():
                            B = {}
                            B["qkv"] = AR.take([128, 3, 128])
                            B["gc4"] = AR.take([128, 4]); B["egc"] = AR.take([128, 2]); B["erem"] = AR.take([128, 2])
                            B["gend"] = AR.take([128, 2]); B["grem"] = AR.take([128, 2])
                            for nm in ("gB", "ET", "Q0f", "P0f", "Mm", "MT", "Qoff32", "Qoff64", "Poff32", "Poff64", "Poff128", "T1s", "T3s", "Xm", "QKm", "UW", "ZT"):
                                B[nm] = AR.take([128, 2, 128])
                            for nm in ("Kdec", "Qdec", "ATb", "bsb", "S0", "S1"):
                                B[nm] = AR.take([128, 2, 64])
                            return B
                        sets = [make_set(), make_set()]
                        bk = Banks(list(range(8)))
                        P.memset("pool", Oacc[:, :, :], 0.0, writes=[("Oacc", i_) for i_ in range(NT)])

                        def phase_a(i, d, B, z):
                            K_ = lambda n_: (n_, z)
                            hd0 = d * 4 + 2 * pair
                            tsl = slice(i * 128, (i + 1) * 128)
                            qkv, gc4, egc, erem, gend, grem = B["qkv"], B["gc4"], B["egc"], B["erem"], B["gend"], B["grem"]
                            gB, ET, Q0f, P0f, Mm, MT, T1s, T3s, Xm = B["gB"], B["ET"], B["Q0f"], B["P0f"], B["Mm"], B["MT"], B["T1s"], B["T3s"], B["Xm"]
                            Qoff = {32: B["Qoff32"], 64: B["Qoff64"]}
                            Poff = {32: B["Poff32"], 64: B["Poff64"], 128: B["Poff128"]}
                            QKm, UW, ZT, Kdec, Qdec, ATb, bsb = B["QKm"], B["UW"], B["ZT"], B["Kdec"], B["Qdec"], B["ATb"], B["bsb"]
                            b1 = bk.next()
                            for j, src, sk in ((0, qT_, "qT_"), (1, kT_, "kT_"), (2, vT_, "vT_")):
                                P.mm(pbank[b1][:, j * 128:(j + 1) * 128], src[:, tsl], identb[:, :], True, True,
                                     reads=[sk, "identb"], writes=[("pb", b1)])
                            P.copy("act", qkv[:, :, :], pbank[b1][:, 0:384].rearrange("p (a b) -> p a b", a=3), reads=[("pb", b1)], writes=[K_("qkv")])
                            b1 = bk.next()
                            gcol = gdec[:, i, 0:8]
                            P.mm(pbank[b1][:, 0:8], TriI[d], gcol, True, True, reads=["gm", "gdec"], writes=[("pb", b1)])
                            P.mm(pbank[b1][:, 8:16], ones128, gcol, True, True, reads=["gm", "gdec"], writes=[("pb", b1)])
                            P.copy("dve", gc4[:, 0:2], pbank[b1][:, hd0:hd0 + 2], reads=[("pb", b1)], writes=[K_("gc4")])
                            P.copy("dve", gc4[:, 2:4], pbank[b1][:, 8 + hd0:8 + hd0 + 2], reads=[("pb", b1)], writes=[K_("gc4")])
                            yield
                            P.act(egc[:, :], gc4[:, 0:2], AF.Exp, reads=[K_("gc4")], writes=[K_("egc")])
                            P.act(gend[:, :], gc4[:, 2:4], AF.Exp, reads=[K_("gc4")], writes=[K_("gend")])
                            P.tt("dve", grem[:, :], gc4[:, 2:4], gc4[:, 0:2], ALU.subtract, reads=[K_("gc4")], writes=[K_("grem")])
                            P.act(erem[:, :], grem[:, :], AF.Exp, reads=[K_("grem")], writes=[K_("erem")])
                            for h in range(2):
                                P.copy("pool", gB[:, h, :], gdec[:, i, hd0 + h:hd0 + h + 1].broadcast_to([128, 128]), reads=["gdec"], writes=[K_("gB")])
                            b1 = bk.next()
                            for h in range(2):
                                o_ = pbank[b1][:, h * 128:(h + 1) * 128]
                                P.mm(o_, gB[:, h, :], TriI[d], True, False, reads=[K_("gB"), "gm"], writes=[("pb", b1)])
                                P.mm(o_, NegTri[d], gB[:, h, :], False, False, reads=[K_("gB"), "gm"], writes=[("pb", b1)])
                                P.mm(o_, identf[:, :], MaskB[d][:, 0:128], False, True, reads=["identf", "gm"], writes=[("pb", b1)])
                            P.act(ET[:, :, :], v3(pbank[b1][:, 0:256]), AF.Exp, reads=[("pb", b1)], writes=[K_("ET")])
                            gbk = [bk.next(), bk.next()]
                            for h in range(2):
                                pr = slice(64 * h, 64 * h + 64)
                                P.mm(pbank[gbk[h]][:, 0:128], kT_[pr, tsl], kT_[pr, tsl], True, True, reads=["kT_"], writes=[("pb", gbk[h])])
                                P.mm(pbank[gbk[h]][:, 128:256], kT_[pr, tsl], qT_[pr, tsl], True, True,
                                     reads=["kT_", "qT_"], writes=[("pb", gbk[h])])
                            yield
                            for h in range(2):
                                P.tt("dve", QKm[:, h, :], pbank[gbk[h]][:, 128:256], ET[:, h, :], ALU.mult,
                                     reads=[("pb", gbk[h]), K_("ET")], writes=[K_("QKm")])
                            P.tt("pool", ET[:, :, :], ET[:, :, :], v3(Strict[d]), ALU.mult, reads=[K_("ET"), "gm"], writes=[K_("ET")])
                            for h in range(2):
                                P.stt(Q0f[:, h, :], pbank[gbk[h]][:, 0:128], beta[:, i, hd0 + h:hd0 + h + 1], ET[:, h, :], ALU.mult, ALU.mult,
                                      reads=[("pb", gbk[h]), "beta", K_("ET")], writes=[K_("Q0f")])
                            P.copy("pool", Xm[:, :, 0:64], qkv[:, 2, :].rearrange("p (a b) -> p a b", a=2), reads=[K_("qkv")], writes=[K_("Xm")])
                            P.tt("pool", Xm[:, :, 64:128], qkv[:, 1, :].rearrange("p (a b) -> p a b", a=2),
                                 egc[:, :].unsqueeze(2).broadcast_to([128, 2, 64]), ALU.mult, reads=[K_("qkv"), K_("egc")], writes=[K_("Xm")])
                            P.tt("pool", Kdec[:, :, :], qkv[:, 1, :].rearrange("p (a b) -> p a b", a=2),
                                 erem[:, :].unsqueeze(2).broadcast_to([128, 2, 64]), ALU.mult, reads=[K_("qkv"), K_("erem")], writes=[K_("Kdec")])
                            P.tt("pool", Qdec[:, :, :], qkv[:, 0, :].rearrange("p (a b) -> p a b", a=2),
                                 egc[:, :].unsqueeze(2).broadcast_to([128, 2, 64]), ALU.mult, reads=[K_("qkv"), K_("egc")], writes=[K_("Qdec")])
                            yield

                            def mm2(lhs, lk, rhs, rk):
                                bx = bk.next()
                                for h in range(2):
                                    P.mm(pbank[bx][:, h * 128:(h + 1) * 128], lhs[:, h, :], rhs[:, h, :] if rhs is not None else identf[:, :], True, True,
                                         reads=[lk, rk], writes=[("pb", bx)])
                                return bx
                            bx = mm2(Q0f, K_("Q0f"), None, "identf")
                            P.copy("act", P0f[:, :, :], v3(pbank[bx][:, 0:256]), reads=[("pb", bx)], writes=[K_("P0f")])
                            for L_ in (32, 64):
                                P.tt("pool", Qoff[L_][:, :, :], Q0f[:, :, :], OFF[L_], ALU.mult, reads=[K_("Q0f"), "gm"], writes=[K_("Qoff%d" % L_)])
                            P.tt("pool", Q0f[:, :, :], Q0f[:, :, :], BD16, ALU.mult, reads=[K_("Q0f"), "gm"], writes=[K_("Q0f")])
                            yield
                            for L_ in (32, 64, 128):
                                P.tt("pool", Poff[L_][:, :, :], P0f[:, :, :], OFF[L_], ALU.mult, reads=[K_("P0f"), "gm"], writes=[K_("Poff%d" % L_)])
                            P.tt("pool", P0f[:, :, :], P0f[:, :, :], BD16, ALU.mult, reads=[K_("P0f"), "gm"], writes=[K_("P0f")])
                            Qk, Pm = Q0f, P0f
                            kQ, kP = K_("Q0f"), K_("P0f")
                            idb = identf[:, :].unsqueeze(1).broadcast_to([128, 2, 128])
                            P.tt("pool", Mm[:, :, :], idb, Pm[:, :, :], ALU.subtract, reads=["identf", kP], writes=[K_("Mm")])
                            P.tt("pool", MT[:, :, :], idb, Qk[:, :, :], ALU.subtract, reads=["identf", kQ], writes=[K_("MT")])
                            yield
                            for lev in range(1, 4):
                                bp = mm2(Qk, kQ, Pm, kP)
                                bq = mm2(Pm, kP, Qk, kQ)
                                P.copy("act", Pm[:, :, :], v3(pbank[bp][:, 0:256]), reads=[("pb", bp)], writes=[kP])
                                P.copy("dve", Qk[:, :, :], v3(pbank[bq][:, 0:256]), reads=[("pb", bq)], writes=[kQ])
                                yield
                                b1_ = mm2(Qk, kQ, Mm, K_("Mm"))
                                b2_ = mm2(Pm, kP, MT, K_("MT"))
                                P.tt("dve", Mm[:, :, :], Mm[:, :, :], v3(pbank[b1_][:, 0:256]), ALU.add, reads=[("pb", b1_), K_("Mm")], writes=[K_("Mm")])
                                P.tt("dve", MT[:, :, :], MT[:, :, :], v3(pbank[b2_][:, 0:256]), ALU.add, reads=[("pb", b2_), K_("MT")], writes=[K_("MT")])
                                yield
                            for L_ in (32, 64, 128):
                                if L_ < 128:
                                    b1_ = mm2(Qoff[L_], K_("Qoff%d" % L_), Mm, K_("Mm"))
                                    P.copy("act", T1s[:, :, :], v3(pbank[b1_][:, 0:256]), reads=[("pb", b1_)], writes=[K_("T1s")])
                                b3_ = mm2(Poff[L_], K_("Poff%d" % L_), MT, K_("MT"))
                                P.copy("dve" if L_ < 128 else "act", T3s[:, :, :], v3(pbank[b3_][:, 0:256]), reads=[("pb", b3_)], writes=[K_("T3s")])
                                yield
                                if L_ < 128:
                                    b1_ = mm2(MT, K_("MT"), T1s, K_("T1s"))
                                b3_ = mm2(Mm, K_("Mm"), T3s, K_("T3s"))
                                if L_ < 128:
                                    P.tt("dve", Mm[:, :, :], Mm[:, :, :], v3(pbank[b1_][:, 0:256]), ALU.subtract, reads=[("pb", b1_), K_("Mm")], writes=[K_("Mm")])
                                P.tt("dve", MT[:, :, :], MT[:, :, :], v3(pbank[b3_][:, 0:256]), ALU.subtract, reads=[("pb", b3_), K_("MT")], writes=[K_("MT")])
                                yield
                            bx = mm2(MT, K_("MT"), Xm, K_("Xm"))
                            bb = beta[:, i, hd0:hd0 + 2].unsqueeze(2).broadcast_to([128, 2, 64])
                            xps = v3(pbank[bx][:, 0:256])
                            P.tt("dve", UW[:, :, 0:64], xps[:, :, 0:64], bb, ALU.mult, reads=[("pb", bx), "beta"], writes=[K_("UW")])
                            P.stt(UW[:, :, 64:128], xps[:, :, 64:128], -1.0, bb, ALU.mult, ALU.mult, reads=[("pb", bx), "beta"], writes=[K_("UW")])
                            yield
                            ba = bk.next()
                            for h in range(2):
                                P.mm(pbank[ba][0:64, h * 64:(h + 1) * 64], UW[:, h, 64:128], Kdec[:, h, :], True, True,
                                     reads=[K_("UW"), K_("Kdec")], writes=[("pb", ba)])
                            for h in range(2):
                                P.mm(pbank[ba][0:64, 128 + h * 64:128 + (h + 1) * 64], Kdec[:, h, :], UW[:, h, 0:64], True, True,
                                     reads=[K_("UW"), K_("Kdec")], writes=[("pb", ba)])
                            bz = bk.next()
                            for h in range(2):
                                o_ = pbank[bz][0:64, h * 128:(h + 1) * 128]
                                P.mm(o_, Qdec[:, h, :], identf[:, :], True, False, reads=[K_("Qdec"), "identf"], writes=[("pb", bz)])
                                P.mm(o_, UW[:, h, 64:128], QKm[:, h, :], False, True, reads=[K_("UW"), K_("QKm")], writes=[("pb", bz)])
                            for h in range(2):
                                P.stt(ATb[0:64, h, :], I64[0:64, h * 64:(h + 1) * 64], gend[0:64, h:h + 1], pbank[ba][0:64, h * 64:(h + 1) * 64],
                                      ALU.mult, ALU.add, reads=["gm", K_("gend"), ("pb", ba)], writes=[K_("AT")])
                            P.copy("dve", bsb[0:64, :, :], v3(pbank[ba][0:64, 128:256]), reads=[("pb", ba)], writes=[K_("bsb")])
                            P.copy("dve", ZT[0:64, :, :], v3(pbank[bz][0:64, 0:256]), reads=[("pb", bz)], writes=[K_("ZT")])
                            yield

                        def phase_b(i, B, z, sidx, need_out):
                            K_ = lambda n_: (n_, z)
                            S_old, S_new = (B["S0"], B["S1"]) if sidx % 2 == 0 else (B["S1"], B["S0"])
                            ko, kn = K_("S%d" % (sidx % 2)), K_("S%d" % ((sidx + 1) % 2))
                            QKm, UW, ZT, ATb, bsb = B["QKm"], B["UW"], B["ZT"], B["ATb"], B["bsb"]
                            if need_out:
                                bo = bk.next()
                                for h in range(2):
                                    o_ = pbank[bo][:, h * 64:(h + 1) * 64]
                                    P.mm(o_, QKm[:, h, :], UW[:, h, 0:64], True, False, reads=[K_("QKm"), K_("UW")], writes=[("pb", bo)])
                                    P.mm(o_, ZT[0:64, h, :], S_old[0:64, h, :], False, True, reads=[K_("ZT"), ko], writes=[("pb", bo)])
                                P.tt("dve", Oacc[:, i, :], Oacc[:, i, :], pbank[bo][:, 0:128], ALU.add, reads=[("pb", bo), ("Oacc", i)], writes=[("Oacc", i)])
                            bs_ = bk.next()
                            for h in range(2):
                                P.mm(pbank[bs_][0:64, h * 64:(h + 1) * 64], ATb[0:64, h, :], S_old[0:64, h, :], True, True,
                                     reads=[K_("AT"), ko], writes=[("pb", bs_)])
                            P.tt("dve", S_new[0:64, :, :], bsb[0:64, :, :], v3(pbank[bs_][0:64, 0:128]), ALU.add,
                                 reads=[("pb", bs_), K_("bsb")], writes=[kn])
                            yield

                        def thread(d, B, z):
                            order = [0, 1] + list(range(2, NT)) if d == 0 else [1, 0] + list(range(NT - 1, 1, -1))
                            P.memset("dve", B["S0"][:, :, :], 0.0, writes=[("S0", z)])
                            for n_, i in enumerate(order):
                                yield from phase_a(i, d, B, z)
                                yield from phase_b(i, B, z, n_, not (last and i < 2))
                        threads = [thread(0, sets[0], 0), thread(1, sets[1], 1)]
                        alive = [True, True]
                        while any(alive):
                            for z in range(2):
                                if alive[z]:
                                    try:
                                        next(threads[z])
                                    except StopIteration:
                                        alive[z] = False
                        if pair == 0:
                            dbg_dump("g_Oacc", Oacc[:, :, :])
                        if gstop >= 3:
                            continue
                        P.barrier()
                        AR.at(G1)
                        ws = WStream(2, 3)
                        sq = AR.take([128, NT, 128])
                        ssum = AR.take([128, NT, 2])
                        ybf = AR.take([128, NT, 128], BF16)
                        mixG = AR.take([128, T], BF16)
                        gsil = AR.take([128, T], BF16)
                        wt, wk = ws.load([(wv_in[:, :, 1536 + pair * 128:1536 + (pair + 1) * 128], 0, 128)], 8)
                        gpb = Banks([0, 1, 2, 3])
                        for tb in tbs_out:
                            t0, tl = TBS[tb]
                            b1 = gpb.next()
                            for k in range(NCH):
                                P.mm(pbank[b1][:, 0:tl], wt[:, k, :], uT[:, k, t0:t0 + tl], k == 0, k == NCH - 1,
                                     reads=[wk, ("uT", tb)], writes=[("pb", b1)])
                            P.act(gsil[:, t0:t0 + tl], pbank[b1][:, 0:tl], AF.Silu, reads=[("pb", b1)], writes=["gsil"])
                        tiles_o = list(range(2, NT)) if last else list(range(NT))
                        t_lo = tiles_o[0]
                        nto = len(tiles_o)
                        P.tt("pool", sq[:, t_lo:NT, :], Oacc[:, t_lo:NT, :], Oacc[:, t_lo:NT, :], ALU.mult, reads=[("Oacc", i) for i in tiles_o], writes=["gsq2"])
                        red_o = ssum[:, t_lo:NT, :]
                        red_i = sq[:, t_lo:NT, :].rearrange("p t (a b) -> p t a b", a=2)
                        P.op("dve", lambda e, red_o=red_o, red_i=red_i: e.tensor_reduce(out=red_o, in_=red_i, axis=mybir.AxisListType.X, op=ALU.add),
                             reads=["gsq2"], writes=["ssum"])
                        P.act(ssum[:, t_lo:NT, :], ssum[:, t_lo:NT, :], AF.Sqrt, bias=EPS, scale=1.0 / 64, reads=["ssum"], writes=["ssum"])
                        P.recip(ssum[:, t_lo:NT, :], ssum[:, t_lo:NT, :], reads=["ssum"], writes=["ssum"])
                        P.tt("dve", sq[:, t_lo:NT, :].rearrange("p t (a b) -> p t a b", a=2), Oacc[:, t_lo:NT, :].rearrange("p t (a b) -> p t a b", a=2),
                             ssum[:, t_lo:NT, :].unsqueeze(3).broadcast_to([128, nto, 2, 64]), ALU.mult,
                             reads=[("Oacc", i) for i in tiles_o] + ["ssum"], writes=["gsq2"])
                        P.tt("pool", ybf[:, t_lo:NT, :].rearrange("p t (a b) -> p t a b", a=2), sq[:, t_lo:NT, :].rearrange("p t (a b) -> p t a b", a=2),
                             gn_bc[:, l, :].unsqueeze(1).unsqueeze(1).broadcast_to([128, nto, 2, 64]), ALU.mult,
                             reads=["gsq2", "smallp"], writes=["ybf"])
                        fb = Banks([4, 5, 6, 7])
                        for i0 in range(t_lo, NT, 4):
                            b1 = fb.next()
                            n4 = min(4, NT - i0)
                            for ii in range(n4):
                                P.mm(pbank[b1][:, ii * 128:(ii + 1) * 128], ybf[:, i0 + ii, :], identb[:, :], True, True, reads=["ybf", "identb"], writes=[("pb", b1)])
                            P.tt("dve", mixG[:, i0 * 128:(i0 + n4) * 128], pbank[b1][:, 0:n4 * 128], gsil[:, i0 * 128:(i0 + n4) * 128], ALU.mult,
                                 reads=[("pb", b1), "gsil"], writes=["mixG"])
                        if pair == 0:
                            dbg_dump("g_mixG", mixG[:, :], BF16)
                        wout_accumulate(ws, l, s, 4 + pair, mixG, "mixG", tbs_out, Banks([0, 1, 2, 3]))

                if do_hyena:
                    P.barrier()
                    AR.reset()
                    hy_conv = sp("hy_conv").rearrange("p (l j m) -> p l j m", l=NL, j=3)
                    hy_bias = sp("hy_bias").rearrange("p (l m) -> p l m", l=NL)
                    hw1 = sp("hw1").rearrange("p (l w) -> p l w", l=NL)
                    hw2 = sp("hw2").rearrange("p (l w) -> p l w", l=NL)
                    hb1, hb2, hfr = sp("hb1"), sp("hb2"), sp("hfr")
                    segs = ([] if last else [(0, CTX)]) + [(CTX, T)]
                    x0c = AR.take([128, 2, T], BF16)
                    uTh = AR.take([128, 2, T], BF16)
                    R0 = AR.off
                    ws = WStream(3, 4)
                    x1c = AR.take([128, T], BF16)
                    raw = AR.take([128, T], BF16)
                    acc = AR.take([128, T])
                    pjb = Banks([6, 7])
                    for cc in range(2):
                        for which, cbase, mbase in (("x1", 2064, 2), ("v", 2320, 4), ("x0", 1808, 0)):
                            col0 = cbase + cc * 128
                            mcol = mbase + cc
                            wt, wk = ws.load([(wv_in[:, :, col0:col0 + 128], 0, 128)], 8)
                            for tb in tbs_out:
                                t0, tl = TBS[tb]
                                b1 = pjb.next()
                                for k in range(NCH):
                                    P.mm(pbank[b1][:, 0:tl], wt[:, k, :], uT[:, k, t0:t0 + tl], k == 0, k == NCH - 1,
                                         reads=[wk, ("uT", tb)], writes=[("pb", b1)])
                                P.copy("act", raw[:, t0:t0 + tl], pbank[b1][:, 0:tl], reads=[("pb", b1)], writes=["hraw"])
                            for (a, b) in segs:
                                P.act(acc[:, a:b], raw[:, a:b], AF.Identity, scale=hy_conv[:, l, 1, mcol:mcol + 1],
                                      reads=["hraw", "smallp"], writes=["hacc"])
                                P.stt(acc[:, a + 1:b], raw[:, a:b - 1], hy_conv[:, l, 0, mcol:mcol + 1], acc[:, a + 1:b], ALU.mult, ALU.add,
                                      reads=["hraw", "hacc", "smallp"], writes=["hacc"])
                                P.stt(acc[:, a:b - 1], raw[:, a + 1:b], hy_conv[:, l, 2, mcol:mcol + 1], acc[:, a:b - 1], ALU.mult, ALU.add,
                                      reads=["hraw", "hacc", "smallp"], writes=["hacc"])
                            a0 = segs[0][0]
                            if which == "x1":
                                P.copy("pool", x1c[:, a0:T], acc[:, a0:T], reads=["hacc"], writes=["x1c"])
                            elif which == "v":
                                P.tt("pool", uTh[:, cc, a0:T], acc[:, a0:T], x1c[:, a0:T], ALU.mult, reads=["hacc", "x1c"], writes=[("uTh", cc)])
                            else:
                                P.copy("pool", x0c[:, cc, a0:T], acc[:, a0:T], reads=["hacc"], writes=[("x0c", cc)])
                    def sin_reduce(dst, arg, ki, kf, n, key):
                        P.ts("dve", ki[0:64, 0:n], arg[0:64, 0:n], 1.0 / (2.0 * np.pi), None, ALU.mult, reads=[key], writes=[key + "ki"])
                        P.copy("pool", kf[0:64, 0:n], ki[0:64, 0:n], reads=[key + "ki"], writes=[key + "kf"])
                        P.stt(arg[0:64, 0:n], kf[0:64, 0:n], -2.0 * np.pi, arg[0:64, 0:n], ALU.mult, ALU.add,
                              reads=[key + "kf", key], writes=[key])
                        P.act(dst[0:64, 0:n], arg[0:64, 0:n], AF.Sin, scale=0.999999, reads=[key], writes=[key + "out"])

                    for (L_, toff, seg0) in ([(LAT, 2, CTX)] + ([] if last else [(CTX, 0, 0)])):
                        hc_ = hyc[L_]
                        ntc = L_ // 128
                        nfc = ntc
                        tbw = min(512, L_)
                        ntb = L_ // tbw
                        P.barrier()
                        AR.at(R0)
                        u_tok = AR.take([128, 18, 256], BF16)
                        R1 = AR.off
                        tiles = list(range(2, 18)) if L_ == LAT else [0, 1]
                        tb4 = Banks([0, 1, 2, 3])
                        for i in tiles:
                            b1 = tb4.next()
                            for cc in range(2):
                                P.mm(pbank[b1][:, cc * 128:(cc + 1) * 128], uTh[:, cc, i * 128:(i + 1) * 128], identb[:, :], True, True,
                                     reads=[("uTh", cc), "identb"], writes=[("pb", b1)])
                            P.copy("act" if i % 2 else "dve", u_tok[:, i, :], pbank[b1][:, 0:256], reads=[("pb", b1)], writes=[("u_tok", i)])
                        P.barrier()
                        AR.at(R1)
                        S_tok = AR.take([128, 16, 256], BF16)
                        D_tok = AR.take([128, 16, 256], BF16)
                        R2 = AR.off
                        hw3 = AR.take([128, 512])
                        P.dma("sp", hw3[0:64, :], hw3_d[l], P.dsem("hyft"), writes=["hw3"])
                        ft = AR.take([128, 512])
                        P.memset("dve", ft[:, :], 0.0, writes=["ft"])
                        arg = AR.take([128, 512])
                        ki = AR.take([128, 512]).bitcast(I32)
                        kf = AR.take([128, 512])
                        hid1 = AR.take([128, 512])
                        hid2 = AR.take([128, 512])
                        win = AR.take([128, 256])
                        hf = AR.take([128, 256])
                        hb = AR.take([128, 256])
                        dft_ = P.dsem("hyft")
                        fb4 = Banks([0, 1, 2, 3])
                        BW = tbw
                        for blk in range(L_ // BW):
                            P.dma("sp", ft[0:33, 0:BW], hc_["featsT"][:, blk * BW:(blk + 1) * BW], dft_, writes=["ft"])
                            b1 = fb4.next()
                            P.mm(pbank[b1][0:64, 0:BW], hw1[0:33, l, :], ft[0:33, 0:BW], True, True, reads=["ft", "smallp"], writes=[("pb", b1)])
                            P.ts("dve", arg[0:64, 0:BW], pbank[b1][0:64, 0:BW], hb1[0:64, l:l + 1], hfr[0:64, l:l + 1], ALU.add, ALU.mult,
                                 reads=[("pb", b1), "smallp"], writes=["harg"])
                            sin_reduce(hid1, arg, ki, kf, BW, "harg")
                            b1 = fb4.next()
                            P.mm(pbank[b1][0:64, 0:BW], hw2[0:64, l, :], hid1[0:64, 0:BW], True, True, reads=["hargout", "smallp"], writes=[("pb", b1)])
                            P.ts("dve", arg[0:64, 0:BW], pbank[b1][0:64, 0:BW], hb2[0:64, l:l + 1], hfr[0:64, l:l + 1], ALU.add, ALU.mult,
                                 reads=[("pb", b1), "smallp"], writes=["harg"])
                            sin_reduce(hid2, arg, ki, kf, BW, "harg")
                            for tt_ in range(BW // 128):
                                tcg = blk * (BW // 128) + tt_
                                b1 = fb4.next()
                                P.mm(pbank[b1][:, 0:512], hid2[0:64, tt_ * 128:(tt_ + 1) * 128], hw3[0:64, :], True, True,
                                     reads=["hargout", "hw3"], writes=[("pb", b1)])
                                P.act(win[:, :], hy_delta[:, :], AF.Exp, scale=hy_negt[L_][:, tcg:tcg + 1], reads=["hyc"], writes=["win"])
                                P.tt("dve", hf[:, :], pbank[b1][:, 0:256], win[:, :], ALU.mult, reads=[("pb", b1), "win"], writes=["hf"])
                                P.tt("dve", hb[:, :], pbank[b1][:, 256:512], win[:, :], ALU.mult, reads=[("pb", b1), "win"], writes=["hb"])
                                if tcg == 0:
                                    P.memset("dve", hb[0:1, :], 0.0, writes=["hb"])
                                P.tt("pool", S_tok[:, tcg, :], hf[:, :], hb[:, :], ALU.add, reads=["hf", "hb"], writes=[("S_tok", tcg)])
                                P.tt("pool", D_tok[:, tcg, :], hf[:, :], hb[:, :], ALU.subtract, reads=["hf", "hb"], writes=[("D_tok", tcg)])
                        dbg_dump("u_tok%d" % L_, u_tok[:, :, :], BF16)
                        dbg_dump("S_tok%d" % L_, S_tok[:, :, :], BF16)
                        dbg_dump("D_tok%d" % L_, D_tok[:, :, :], BF16)
                        P.barrier()
                        AR.at(R2)
                        Yre = AR.take([128, nfc, 256], BF16)
                        Yim = AR.take([128, nfc, 256], BF16)
                        R3 = AR.off
                        cb = [AR.take([128, ntc, 128], BF16) for _ in range(2)]
                        sbt = [AR.take([128, ntc, 128], BF16) for _ in range(2)]
                        abs_ = AR.take([128, 512])
                        tm = [AR.take([128, 256]) for _ in range(2)]
                        dfw = [P.dsem("hyfw0"), P.dsem("hyfw1")]
                        for fc in range(nfc):
                            bi = fc % 2
                            P.dma("sp", cb[bi][:, :, :], hc_["fwdC"][fc], dfw[bi], writes=[("cb", bi)])
                            P.dma("sp", sbt[bi][:, :, :], hc_["fwdS"][fc], dfw[bi], writes=[("sbt", bi)])
                            bpq = 0 + 2 * bi
                            bab = 1 + 2 * bi
                            for (bk_, c0_, mat_, mk_, rhs_, rk_) in ((bpq, 0, cb, "cb", u_tok, "u_tok"), (bpq, 256, sbt, "sbt", u_tok, "u_tok"),
                                                                  (bab, 0, cb, "cb", S_tok, "S_tok"), (bab, 256, sbt, "sbt", D_tok, "D_tok")):
                                for tc in range(ntc):
                                    ti = toff + tc if rk_ == "u_tok" else tc
                                    P.mm(pbank[bk_][:, c0_:c0_ + 256], mat_[bi][:, tc, :], rhs_[:, ti, :], tc == 0, tc == ntc - 1,
                                         reads=[(mk_, bi), (rk_, ti)], writes=[("pb", bk_)])
                            P.copy("act", abs_[:, :], pbank[bab][:, :], reads=[("pb", bab)], writes=["abs"])
                            Pp, Qp = pbank[bpq][:, 0:256], pbank[bpq][:, 256:512]
                            As, Bs = abs_[:, 0:256], abs_[:, 256:512]
                            P.tt("dve", tm[0][:, :], Pp, As, ALU.mult, reads=[("pb", bpq), "abs"], writes=[("tm", 0)])
                            P.tt("dve", tm[1][:, :], Qp, Bs, ALU.mult, reads=[("pb", bpq), "abs"], writes=[("tm", 1)])
                            P.tt("pool", Yre[:, fc, :], tm[0][:, :], tm[1][:, :], ALU.subtract, reads=[("tm", 0), ("tm", 1)], writes=[("Y", fc)])
                            P.tt("dve", tm[0][:, :], Pp, Bs, ALU.mult, reads=[("pb", bpq), "abs"], writes=[("tm", 0)])
                            P.tt("dve", tm[1][:, :], Qp, As, ALU.mult, reads=[("pb", bpq), "abs"], writes=[("tm", 1)])
                            P.tt("pool", Yim[:, fc, :], tm[0][:, :], tm[1][:, :], ALU.add, reads=[("tm", 0), ("tm", 1)], writes=[("Y", fc)])
                        dbg_dump("Yre%d" % L_, Yre[:, :, :], BF16)
                        dbg_dump("Yim%d" % L_, Yim[:, :, :], BF16)
                        P.barrier()
                        AR.at(R3)
                        FG = min(4, nfc)
                        ci = [AR.take([128, FG, tbw], BF16) for _ in range(2)]
                        si = [AR.take([128, FG, tbw], BF16) for _ in range(2)]
                        ytmp = [AR.take([128, 512]) for _ in range(2)]
                        AR.at(R0)
                        ws = WStream(2, 3)
                        aTh = [AR.take([128, T], BF16) for _ in range(2)]
                        assert AR.off <= R2
                        div = [P.dsem("hyiv0"), P.dsem("hyiv1")]
                        nld = 0
                        for tb_ in range(ntb):
                            banks = (4 + 2 * (tb_ % 2), 5 + 2 * (tb_ % 2))
                            for fg in range(nfc // FG):
                                bi = nld % 2
                                nld += 1
                                P.dma("sp", ci[bi][:, :, :], hc_["invC"][tb_, :, fg * FG:(fg + 1) * FG, :], div[bi], writes=[("ci", bi)])
                                P.dma("sp", si[bi][:, :, :], hc_["invS"][tb_, :, fg * FG:(fg + 1) * FG, :], div[bi], writes=[("si", bi)])
                                for fi in range(FG):
                                    fc = fg * FG + fi
                                    for cc in range(2):
                                        P.mm(pbank[banks[cc]][:, 0:tbw], Yre[:, fc, cc * 128:(cc + 1) * 128], ci[bi][:, fi, :], fc == 0, False,
                                             reads=[("Y", fc), ("ci", bi)], writes=[("pb", banks[cc])])
                                        P.mm(pbank[banks[cc]][:, 0:tbw], Yim[:, fc, cc * 128:(cc + 1) * 128], si[bi][:, fi, :], False, fc == nfc - 1,
                                             reads=[("Y", fc), ("si", bi)], writes=[("pb", banks[cc])])
                            t0 = seg0 + tb_ * tbw
                            for cc in range(2):
                                yt = ytmp[cc]
                                P.stt(yt[:, 0:tbw], uTh[:, cc, t0:t0 + tbw], hy_bias[:, l, cc:cc + 1], pbank[banks[cc]][:, 0:tbw], ALU.mult, ALU.add,
                                      reads=[("uTh", cc), "smallp", ("pb", banks[cc])], writes=[("ytmp", cc)])
                                P.tt("pool", aTh[cc][:, t0:t0 + tbw], yt[:, 0:tbw], x0c[:, cc, t0:t0 + tbw], ALU.mult,
                                     reads=[("ytmp", cc), ("x0c", cc)], writes=[("aTh", cc)])
                        dbg_dump("aTh%d" % L_, aTh[0][:, :], BF16)
                        dbg_dump("x0c%d" % L_, x0c[:, :, :], BF16)
                        dbg_dump("uTh%d" % L_, uTh[:, :, :], BF16)
                        obk = Banks([0, 1, 2, 3])
                        tbs_seg = [1, 2, 3, 4] if L_ == LAT else [0]
                        for cc in range(2):
                            wout_accumulate(ws, l, s, 6 + cc, aTh[cc], ("aTh", cc), tbs_seg, obk)

                if do_ffn:
                    P.barrier()
                    AR.reset()
                    norm_modulate(l, s, norm2, 4, 3, tbs_out, Banks([6, 7]))
                    P.barrier()
                    AR.reset()
                    ws = WStream(3, 4)
                    JG = 4
                    hid = AR.take([128, JG, T], BF16)
                    rawg = AR.take([128, T], BF16)
                    rawv = AR.take([128, T], BF16)
                    accg = AR.take([128, T])
                    accv = AR.take([128, T])
                    wd = [AR.take([128, JG, D], BF16) for _ in range(1)]
                    wv_up = wview(ffn_up, l)
                    upb = Banks([0, 1, 2, 3])
                    dnb = Banks([4, 5, 6, 7])
                    segs = ([] if last else [(0, CTX)]) + [(CTX, T)]
                    for j0 in range(0, NJ, JG):
                        js = list(range(j0, min(NJ, j0 + JG)))
                        for jj, j in enumerate(js):
                            wg, wgk = ws.load([(wv_up[:, :, j * 128:(j + 1) * 128], 0, 128)], 8)
                            wvv, wvk = ws.load([(wv_up[:, :, DFF + j * 128:DFF + (j + 1) * 128], 0, 128)], 8)
                            for tb in tbs_out:
                                t0, tl = TBS[tb]
                                for (wt, wk, raw, rk) in ((wg, wgk, rawg, "rawg"), (wvv, wvk, rawv, "rawv")):
                                    b1 = upb.next()
                                    for k in range(NCH):
                                        P.mm(pbank[b1][:, 0:tl], wt[:, k, :], uT[:, k, t0:t0 + tl], k == 0, k == NCH - 1,
                                             reads=[wk, ("uT", tb)], writes=[("pb", b1)])
                                    P.copy("act", raw[:, t0:t0 + tl], pbank[b1][:, 0:tl], reads=[("pb", b1)], writes=[rk])
                            for (raw, rk, acc, ak, mcol) in ((rawg, "rawg", accg, "accg", j), (rawv, "rawv", accv, "accv", NJ + j)):
                                w0 = ffn_conv[:, l, 0, mcol:mcol + 1]
                                w1 = ffn_conv[:, l, 1, mcol:mcol + 1]
                                w2 = ffn_conv[:, l, 2, mcol:mcol + 1]
                                for (a, b) in segs:
                                    P.act(acc[:, a:b], raw[:, a:b], AF.Identity, scale=w1, reads=[rk, "smallp"], writes=[ak])
                                    P.stt(acc[:, a + 1:b], raw[:, a:b - 1], w0, acc[:, a + 1:b], ALU.mult, ALU.add,
                                          reads=[rk, ak, "smallp"], writes=[ak])
                                    P.stt(acc[:, a:b - 1], raw[:, a + 1:b], w2, acc[:, a:b - 1], ALU.mult, ALU.add,
                                          reads=[rk, ak, "smallp"], writes=[ak])
                            a0 = segs[0][0]
                            P.act(accg[:, a0:T], accg[:, a0:T], AF.Silu, reads=["accg"], writes=["accg"])
                            P.tt("pool", hid[:, jj, a0:T], accg[:, a0:T], accv[:, a0:T], ALU.mult,
                                 reads=["accg", "accv"], writes=[("hid", jj)])
                        nj = len(js)
                        dtl = wd[0]
                        for m in range(NCH):
                            pass
                        for jj, j in enumerate(js):
                            wt, wk = ws.load([(ffn_down[l, j * 128:(j + 1) * 128, :].rearrange("p (k n) -> p k n", k=8), 0, 128)], 8)
                            P.copy("pool", dtl[:, jj, :].rearrange("p (k n) -> p k n", k=8), wt[:, :, :], reads=[wk], writes=[("wd", jj)])
                        for m in range(NCH):
                            for tb in tbs_out:
                                t0, tl = TBS[tb]
                                b1 = dnb.next()
                                for jj in range(nj):
                                    P.mm(pbank[b1][:, 0:tl], dtl[:, jj, m * 128:(m + 1) * 128], hid[:, jj, t0:t0 + tl], jj == 0, jj == nj - 1,
                                         reads=[("wd", jj), ("hid", jj)], writes=[("pb", b1)])
                                P.stt(hT[:, m, t0:t0 + tl], pbank[b1][:, 0:tl], mod_ap(l, 5, tb_j(tb, s))[:, m:m + 1],
                                      hT[:, m, t0:t0 + tl], ALU.mult, ALU.add,
                                      reads=[("pb", b1), "modT", ("hT", m, tb)], writes=[("hT", m, tb)])

            P.barrier()
            AR.reset()
            if raw_out:
                for k in range(NCH):
                    P.dma("sp", out_d[s, k * 128:(k + 1) * 128, :], hT[:, k, :], dout, reads=[("hT", k, tb) for tb in range(5)])
            else:
                sq = [AR.take([128, 512], BF16) for _ in range(2)]
                rs = [AR.take([128, 512]) for _ in range(2)]
                ob = [AR.take([128, 512]) for _ in range(4)]
                fb = Banks([6, 7])
                cnt = 0
                oc = 0
                for tb in (1, 2, 3, 4):
                    t0, tl = TBS[tb]
                    pbk = fb.next()
                    ps = pbank[pbk]
                    for k in range(NCH):
                        q = sq[cnt % 2]
                        cnt += 1
                        P.act(q[:, 0:tl], hT[:, k, t0:t0 + tl], AF.Square, reads=[("hT", k, tb)], writes=[("sq", id(q))])
                        P.mm(ps[:, 0:tl], ones_mean[:, :], q[:, 0:tl], k == 0, k == NCH - 1,
                             reads=[("sq", id(q)), "ones_mean"], writes=[("pb", pbk)])
                    r = rs[tb % 2]
                    P.act(r[:, 0:tl], ps[:, 0:tl], AF.Sqrt, bias=EPS, scale=1.0, reads=[("pb", pbk)], writes=[("rs", tb % 2)])
                    P.recip(r[:, 0:tl], r[:, 0:tl], reads=[("rs", tb % 2)], writes=[("rs", tb % 2)])
                    for k in range(NCH):
                        o = ob[oc % 4]
                        P.stt(o[:, 0:tl], hT[:, k, t0:t0 + tl], fnorm[:, k:k + 1], r[:, 0:tl], ALU.mult, ALU.mult,
                              reads=[("hT", k, tb), ("rs", tb % 2), "smallp"], writes=[("ob", oc % 4)])
                        P.dma("sp", out_d[s, k * 128:(k + 1) * 128, t0 - CTX:t0 - CTX + tl], o[:, 0:tl], dout, reads=[("ob", oc % 4)])
                        oc += 1
        P.emit(final_waits=[dout, ddbg])
    return nc


CFG = {}


def prepare(inputs, cfg):
    NS = cfg.get("NS", 2)
    ncores = cfg.get("ncores", 8)
    x = np.asarray(inputs["x"], np.float32)
    ctx = np.asarray(inputs["ctx"], np.float32)
    c = np.asarray(inputs["c"], np.float32)
    c_ctx = np.asarray(inputs["c_ctx"], np.float32)
    cos, sin, RT = rope_tables()
    shared = {
        "w_ada": np.ascontiguousarray(inputs["w_ada"], dtype=np.float32),
        "w_in": np.ascontiguousarray(inputs["w_in"], dtype=np.float32),
        "w_out": np.ascontiguousarray(inputs["w_out"], dtype=np.float32),
        "ffn_up": np.ascontiguousarray(inputs["ffn_up"], dtype=np.float32),
        "ffn_down": np.ascontiguousarray(inputs["ffn_down"], dtype=np.float32),
        "ropecs": np.stack([cos, sin]).astype(np.float32),
        "ropeR": RT,
        "ident": np.eye(128, dtype=np.float32),
        "hy_delta": hyena_deltas(),
        "gmask": gdn_masks(),
        "hyena_w3": np.ascontiguousarray(inputs["hyena_w3"], dtype=np.float32),
    }
    for L_ in (LAT, CTX):
        for k_, v_ in hyena_consts(L_).items():
            shared["hy_%s%d" % (k_, L_)] = v_
    in_maps = []
    cols = None
    for core in range(ncores):
        bs = [core * NS + i for i in range(NS)]
        h0 = np.stack([np.ascontiguousarray(np.concatenate([ctx[b], x[b]], axis=0).T) for b in bs])
        cv = np.stack([c[bs[0]], c[bs[-1]], c_ctx])
        pk = small_layout(inputs, cv)
        cols = pk.cols
        m = dict(shared)
        m["h0"] = h0
        m["smallp"] = pk.pack()
        in_maps.append(m)
    return in_maps, cols


def kernel(**inputs):
    cfg = dict(CFG)
    NS = cfg.get("NS", 2)
    ncores = cfg.get("ncores", 8)
    in_maps, cols = prepare(inputs, cfg)
    cfg["nsmall"] = in_maps[0]["smallp"].shape[1]
    nc = build(cfg, cols)
    res = run_bass_kernel_spmd(nc, in_maps, core_ids=list(range(ncores)))
    outs = []
    for core in range(ncores):
        o = res.results[core]["out"]
        for i in range(NS):
            outs.append(np.ascontiguousarray(o[i].T))
    return np.stack(outs).astype(np.float32)
```

```python
import numpy as np
import ml_dtypes
from contextlib import ExitStack
import concourse.bass as bass
import concourse.mybir as mybir
from concourse.bass_utils import run_bass_kernel_spmd

F32 = mybir.dt.float32
BF16 = mybir.dt.bfloat16
I32 = mybir.dt.int32
AF = mybir.ActivationFunctionType
ALU = mybir.AluOpType

ENGS = ("pe", "act", "dve", "pool", "sp")

D = 1024
NCH = 8
CTX = 256
LAT = 2048
T = CTX + LAT
NL = 4
DFF = 2816
NJ = DFF // 128
INW = 2576
EPS = 1e-6
TBS = [(0, 256), (256, 512), (768, 512), (1280, 512), (1792, 512)]
ARENA_WORDS = 20800


class DSem:
    def __init__(self, name):
        self.name = name
        self.total = 0
        self.h = None


class Prog:
    def __init__(self, nc):
        self.nc = nc
        self.ops = {e: [] for e in ENGS}
        self.cnt = {e: 0 for e in ENGS}
        self.waited = {e: {} for e in ENGS}
        self.lastw = {}
        self.readers = {}
        self.dsems = {}

    def dsem(self, name):
        if name not in self.dsems:
            self.dsems[name] = DSem(name)
        return self.dsems[name]

    def _need(self, eng, tok, waits):
        if tok is None:
            return
        kind, key, val = tok
        if kind == "e" and key == "pe" and eng == "pe":
            return
        if kind == "d":
            val = key.total
        cur = self.waited[eng].get(key, 0)
        if val > cur:
            self.waited[eng][key] = val
            waits[key] = (kind, key, val)

    def op(self, eng, fn, reads=(), writes=(), dsem=None):
        waits = {}
        for k in reads:
            self._need(eng, self.lastw.get(k), waits)
        for k in writes:
            self._need(eng, self.lastw.get(k), waits)
            for t in self.readers.get(k, ()):
                self._need(eng, t, waits)
        if dsem is not None:
            dsem.total += 16
            tok = ("d", dsem, dsem.total)
        else:
            self.cnt[eng] += 1
            tok = ("e", eng, self.cnt[eng])
        for k in reads:
            self.readers.setdefault(k, []).append(tok)
        for k in writes:
            self.lastw[k] = tok
            self.readers[k] = []
        self.ops[eng].append((list(waits.values()), fn, dsem))
        return tok

    def barrier(self):
        for eng in ENGS:
            waits = {}
            for e2 in ENGS:
                if self.cnt[e2] > 0:
                    self._need(eng, ("e", e2, self.cnt[e2]), waits)
            for d in self.dsems.values():
                if d.total > 0:
                    self._need(eng, ("d", d, d.total), waits)
            if waits:
                self.ops[eng].append((list(waits.values()), None, None))
        self.lastw = {}
        self.readers = {}

    def dma(self, eng, out, in_, dsem, reads=(), writes=()):
        return self.op(eng, lambda e: e.dma_start(out=out, in_=in_), reads, writes, dsem=dsem)

    def mm(self, out, lhsT, rhs, start, stop, reads=(), writes=()):
        return self.op("pe", lambda e: e.matmul(out, lhsT=lhsT, rhs=rhs, start=start, stop=stop), reads, writes)

    def transpose(self, out, in_, ident, reads=(), writes=()):
        return self.op("pe", lambda e: e.transpose(out=out, in_=in_, identity=ident), reads, writes)

    def act(self, out, in_, func, reads=(), writes=(), bias=None, scale=None):
        kw = {}
        if bias is not None:
            kw["bias"] = bias
        if scale is not None:
            kw["scale"] = scale
        return self.op("act", lambda e: e.activation(out=out, in_=in_, func=func, **kw), reads, writes)

    def tt(self, eng, out, in0, in1, op, reads=(), writes=()):
        return self.op(eng, lambda e: e.tensor_tensor(out=out, in0=in0, in1=in1, op=op), reads, writes)

    def ts(self, eng, out, in0, s1, s2, op0, op1=None, reads=(), writes=()):
        if op1 is None:
            return self.op(eng, lambda e: e.tensor_scalar(out=out, in0=in0, scalar1=s1, scalar2=None, op0=op0), reads, writes)
        return self.op(eng, lambda e: e.tensor_scalar(out=out, in0=in0, scalar1=s1, scalar2=s2, op0=op0, op1=op1), reads, writes)

    def stt(self, out, in0, scalar, in1, op0, op1, reads=(), writes=()):
        return self.op("dve", lambda e: e.scalar_tensor_tensor(out=out, in0=in0, scalar=scalar, in1=in1, op0=op0, op1=op1), reads, writes)

    def copy(self, eng, out, in_, reads=(), writes=()):
        if eng == "act":
            return self.op("act", lambda e: e.copy(out=out, in_=in_), reads, writes)
        return self.op(eng, lambda e: e.tensor_copy(out=out, in_=in_), reads, writes)

    def memset(self, eng, ap, val, writes=()):
        return self.op(eng, lambda e: e.memset(ap, val), (), writes)

    def recip(self, out, in_, reads=(), writes=()):
        return self.op("dve", lambda e: e.reciprocal(out=out, in_=in_), reads, writes)

    def emit(self, final_waits=()):
        nc = self.nc
        with ExitStack() as st:
            esem = {e: st.enter_context(nc.semaphore("es_" + e)) for e in ENGS}
            for d in self.dsems.values():
                d.h = st.enter_context(nc.semaphore("ds_" + d.name))
            block = st.enter_context(nc.Block())

            def run(engname, e):
                for waits, fn, dsem in self.ops[engname]:
                    for kind, key, val in waits:
                        e.wait_ge(esem[key] if kind == "e" else key.h, val)
                    if fn is None:
                        continue
                    ins = fn(e)
                    if dsem is not None:
                        ins.then_inc(dsem.h, 16)
                    else:
                        ins.then_inc(esem[engname], 1)
                if engname == "sp":
                    for d in final_waits:
                        e.wait_ge(d.h, d.total)

            @block.tensor
            def _(e):
                run("pe", e)

            @block.scalar
            def _(e):
                run("act", e)

            @block.vector
            def _(e):
                run("dve", e)

            @block.gpsimd
            def _(e):
                run("pool", e)

            @block.sync
            def _(e):
                run("sp", e)


class Packer:
    def __init__(self):
        self.cols = {}
        self.parts = []
        self.n = 0

    def add(self, name, arr):
        arr = np.ascontiguousarray(arr, dtype=np.float32).reshape(128, -1)
        self.cols[name] = (self.n, arr.shape[1])
        self.parts.append(arr)
        self.n += arr.shape[1]

    def pack(self):
        return np.concatenate(self.parts, axis=1)


def fm(v):
    v = np.asarray(v, dtype=np.float32)
    lead = v.shape[:-1]
    n = v.shape[-1] // 128
    v = v.reshape(lead + (n, 128))
    return np.moveaxis(v, -1, 0)


def small_layout(inputs, cvecs):
    pk = Packer()
    cT = np.zeros((128, 8, 4), np.float32)
    cT[:, :, 0:3] = np.moveaxis(fm(cvecs), 1, 2)
    pk.add("cT", cT)
    b = fm(inputs["b_ada"])
    pk.add("b_ada", np.repeat(b[:, :, :, None], 4, axis=3))
    pk.add("norm1", fm(inputs["norm1"]))
    pk.add("norm2", fm(inputs["norm2"]))
    pk.add("final_norm", fm(inputs["final_norm"]))
    pk.add("qg", np.tile(np.asarray(inputs["q_gain"], np.float32).T, (2, 1)))
    pk.add("kg", np.tile(np.asarray(inputs["k_gain"], np.float32).T, (2, 1)))
    pk.add("ffn_conv", fm(inputs["ffn_conv"]))
    pk.add("gdn_conv", fm(inputs["gdn_conv"]))
    rep = lambda a: np.broadcast_to(np.asarray(a, np.float32).reshape(1, NL, -1), (128, NL, np.asarray(a).reshape(NL, -1).shape[1]))
    pk.add("alog_bc", rep(inputs["gdn_a_log"]))
    pk.add("dtb_bc", rep(inputs["gdn_dt_bias"]))
    pk.add("gn_bc", rep(inputs["gdn_norm"]))
    pk.add("hy_conv", fm(inputs["hyena_conv"]))
    pk.add("hy_bias", fm(np.asarray(inputs["hyena_bias"])[:, 0, :]))
    def rows(a, n):
        a = np.asarray(a, np.float32)
        o = np.zeros((128,) + (a.shape[0],) + a.shape[2:], np.float32)
        o[:n] = np.moveaxis(a, 1, 0)
        return o
    pk.add("hw1", rows(inputs["hyena_w1"], 33))
    pk.add("hw2", rows(inputs["hyena_w2"], 64))
    pk.add("hb1", rows(np.asarray(inputs["hyena_b1"])[:, :, None], 64))
    pk.add("hb2", rows(np.asarray(inputs["hyena_b2"])[:, :, None], 64))
    pk.add("hfr", rows(np.asarray(inputs["hyena_freq"])[:, :, None], 64))
    return pk


def rope_tables():
    inv = 10000.0 ** (-np.arange(16, dtype=np.float64) / 16.0)
    t = np.arange(LAT)
    row = (t // 64).astype(np.float64)
    col = (t % 64).astype(np.float64)
    cos = np.zeros((128, LAT), np.float64)
    sin = np.zeros((128, LAT), np.float64)
    for p in range(128):
        dd = p % 64
        pos = row if dd < 32 else col
        ang = pos * inv[dd % 16]
        cos[p] = np.cos(ang)
        sin[p] = np.sin(ang)
    R = np.zeros((128, 128), np.float32)
    for blk in range(4):
        for i in range(16):
            R[blk * 32 + i, blk * 32 + i + 16] = -1.0
            R[blk * 32 + i + 16, blk * 32 + i] = 1.0
    return cos.astype(np.float32), sin.astype(np.float32), np.ascontiguousarray(R.T)


def hyena_consts(L):
    f32 = np.float32
    n = L
    t = np.linspace(0.0, 1.0, n, dtype=f32)
    omega = (f32(2.0 * np.pi) * np.arange(n, dtype=f32) / f32(n)).astype(f32)
    bands = np.linspace(1e-4, 15, 16, dtype=f32)
    ang = (bands[None, :] * omega[:, None]).astype(f32)
    feats = np.concatenate([t[:, None], np.cos(ang), -np.sin(ang)], axis=1).astype(f32)
    ntc = L // 128
    negt = np.ascontiguousarray((-t).reshape(ntc, 128).T)
    nfc = L // 128
    tbw = min(512, L)
    ntb = L // tbw
    ff = np.arange(L, dtype=np.int64)
    tt = np.arange(L, dtype=np.int64)
    k = ((2 * ff[None, :] + 1) * tt[:, None]) % (4 * L)
    a = 2.0 * np.pi * k.astype(np.float64) / (4.0 * L)
    C = np.cos(a)
    S = np.sin(a)
    bf = ml_dtypes.bfloat16
    fwdC = C.reshape(ntc, 128, nfc, 128).transpose(2, 1, 0, 3).astype(bf)
    fwdS = S.reshape(ntc, 128, nfc, 128).transpose(2, 1, 0, 3).astype(bf)
    invC = (C.T / L).reshape(nfc, 128, ntb, tbw).transpose(2, 1, 0, 3).astype(bf)
    invS = (S.T / L).reshape(nfc, 128, ntb, tbw).transpose(2, 1, 0, 3).astype(bf)
    return dict(featsT=np.ascontiguousarray(feats.T), negt=negt,
                fwdC=np.ascontiguousarray(fwdC), fwdS=np.ascontiguousarray(fwdS),
                invC=np.ascontiguousarray(invC), invS=np.ascontiguousarray(invS))


def gdn_masks():
    idx = np.arange(128)
    out = []
    for d in range(2):
        if d == 0:
            eo = idx[:, None] <= idx[None, :]
            st = idx[:, None] < idx[None, :]
        else:
            eo = idx[:, None] >= idx[None, :]
            st = idx[:, None] > idx[None, :]
        tri = eo.astype(np.float32)
        mb = np.where(eo, 0.0, -30000.0).astype(np.float32)
        stf = st.astype(np.float32)
        out += [tri, -tri, mb, mb, stf, stf]
    same = lambda n: (idx[:, None] // n) == (idx[None, :] // n)
    bd16 = same(16).astype(np.float32)
    out += [bd16, bd16]
    for n in (32, 64, 128):
        off = (same(n) & ~same(n // 2)).astype(np.float32)
        out += [off, off]
    i64 = np.zeros((128, 128), np.float32)
    i64[:64, 0:64] = np.eye(64)
    i64[:64, 64:128] = np.eye(64)
    out += [i64, np.ones((128, 128), np.float32)]
    return np.ascontiguousarray(np.concatenate(out, axis=1))


def hyena_deltas():
    f32 = np.float32
    max_decay = np.log(1e-2) / 0.3
    min_decay = np.log(1e-2) / 1.5
    d = np.abs(np.linspace(min_decay, max_decay, 256, dtype=f32)).astype(f32)
    return np.ascontiguousarray(np.broadcast_to(d[None, :], (128, 256)))

def build(cfg, cols):
    NS = cfg.get("NS", 2)
    LAYERS = cfg.get("layers", NL)
    do_attn = cfg.get("attn", True)
    do_ffn = cfg.get("ffn", True)
    raw_out = cfg.get("raw_out", False)
    nsmall = cfg["nsmall"]

    nc = bass.Bass("TRN2", target_bir_lowering=False)
    P = Prog(nc)

    def dram(name, shape, dt=F32, kind="ExternalInput"):
        return nc.dram_tensor(name, shape, dt, kind=kind).ap()

    h0 = dram("h0", [NS, D, T])
    smallp_d = dram("smallp", [128, nsmall])
    w_ada = dram("w_ada", [NL, D, 6 * D])
    w_in = dram("w_in", [NL, D, INW])
    w_out = dram("w_out", [NL, D, D])
    ffn_up = dram("ffn_up", [NL, D, 2 * DFF])
    ffn_down = dram("ffn_down", [NL, DFF, D])
    ropecs_d = dram("ropecs", [2, 128, LAT])
    ropeR_d = dram("ropeR", [128, 128])
    ident_d = dram("ident", [128, 128])
    hyc = {}
    for L_ in (LAT, CTX):
        nb = L_ // 128
        tbw = min(512, L_)
        hyc[L_] = dict(featsT=dram("hy_featsT%d" % L_, [33, L_]), negt=dram("hy_negt%d" % L_, [128, nb]),
                       fwdC=dram("hy_fwdC%d" % L_, [nb, 128, nb, 128], BF16), fwdS=dram("hy_fwdS%d" % L_, [nb, 128, nb, 128], BF16),
                       invC=dram("hy_invC%d" % L_, [L_ // tbw, 128, nb, tbw], BF16), invS=dram("hy_invS%d" % L_, [L_ // tbw, 128, nb, tbw], BF16))
    hy_delta_d = dram("hy_delta", [128, 256])
    gmask_d = dram("gmask", [128, 2816])
    hw3_d = dram("hyena_w3", [NL, 64, 512])
    do_hyena = cfg.get("hyena", True)
    do_gdn = cfg.get("gdn", True)
    if raw_out:
        out_d = dram("out", [NS, D, T], kind="ExternalOutput")
    else:
        out_d = dram("out", [NS, D, LAT], kind="ExternalOutput")

    with ExitStack() as st:
        def sb(name, shape, dt=F32):
            return st.enter_context(nc.sbuf_tensor(name, shape, dt))

        hT = sb("hT", [128, NCH, T])
        uT = sb("uT", [128, NCH, T], BF16)
        smallp = sb("smallp_sb", [128, nsmall])
        modT = sb("modT", [128, NL, 48, 4])
        AB = sb("AB", [128, 4, 8])
        scT = sb("scT", [128, 8, 4])
        identf = sb("identf", [128, 128])
        identb = sb("identb", [128, 128], BF16)
        ones_mean = sb("ones_mean", [128, 128], BF16)
        bd_mean = sb("bd_mean", [128, 128], BF16)
        ones_col = sb("ones_col", [128, 64], BF16)
        ropeRf = sb("ropeRf", [128, 128])
        ropeRb = sb("ropeRb", [128, 128], BF16)
        hy_negt = {LAT: sb("hy_negtL", [128, 16]), CTX: sb("hy_negtC", [128, 2])}
        hy_delta = sb("hy_delta_sb", [128, 256])
        arena = sb("arena", [128, ARENA_WORDS])
        pbank = [st.enter_context(nc.psum_tensor("pb%d" % i, [128, 512], F32)) for i in range(8)]

        def sp(name):
            o, n = cols[name]
            return smallp[:, o:o + n]

        class Arena:
            def __init__(self):
                self.off = 0

            def reset(self):
                self.off = 0

            def at(self, off):
                self.off = off

            def take(self, shape, dt=F32):
                n = int(np.prod(shape[1:]))
                words = n if dt == F32 else (n + 1) // 2
                a = arena[:, self.off:self.off + words]
                self.off += words
                assert self.off <= ARENA_WORDS, ("arena overflow", self.off)
                if dt != F32:
                    a = a.bitcast(dt)
                    a = a[:, 0:n]
                if len(shape) == 3:
                    a = a.rearrange("p (a b) -> p a b", a=shape[1])
                elif len(shape) == 4:
                    a = a.rearrange("p (a b c) -> p a b c", a=shape[1], b=shape[2])
                return a

        AR = Arena()
        dbg_on = cfg.get("dbg", False)
        ddbg = P.dsem("dbg")

        def dbg_dump(name, ap, dt=F32):
            if not dbg_on or name in dbg_done:
                return
            dbg_done.add(name)
            P.barrier()
            shp = list(ap.shape)
            dd = nc.dram_tensor("dbg_" + name, shp, dt, kind="ExternalOutput").ap()
            P.dma("sp", dd, ap, ddbg)
            P.barrier()
        dbg_done = set()

        class Banks:
            def __init__(self, ids):
                self.ids = list(ids)
                self.i = 0

            def next(self):
                b = self.ids[self.i % len(self.ids)]
                self.i += 1
                return b

        class WStream:
            def __init__(self, nstg, nwb):
                self.stg = [AR.take([128, 8, 128]) for _ in range(nstg)]
                self.wb = [AR.take([128, 8, 128], BF16) for _ in range(nwb)]
                self.ds = [P.dsem("wstg%d" % i) for i in range(nstg)]
                self.i = 0
                self.j = 0

            def load(self, pieces, nk):
                si = self.i % len(self.stg)
                wi = self.j % len(self.wb)
                self.i += 1
                self.j += 1
                stg, wb = self.stg[si], self.wb[wi]
                for src, co, w in pieces:
                    P.dma("sp", stg[:, 0:nk, co:co + w], src, self.ds[si], writes=[("stg", si)])
                width = max(co + w for _, co, w in pieces)
                P.copy("pool", wb[:, 0:nk, 0:width], stg[:, 0:nk, 0:width], reads=[("stg", si)], writes=[("wb", wi)])
                return wb, ("wb", wi)

        def wview(w, l):
            return w[l].rearrange("(k p) n -> p k n", p=128)

        dc = P.dsem("const")
        P.dma("sp", smallp[:, :], smallp_d[:, :], dc, writes=["smallp"])
        P.dma("sp", identf[:, :], ident_d[:, :], dc, writes=["identf"])
        P.dma("sp", ropeRf[:, :], ropeR_d[:, :], dc, writes=["ropeRf"])
        P.dma("sp", hy_negt[LAT][:, :], hyc[LAT]["negt"][:, :], dc, writes=["hyc"])
        P.dma("sp", hy_negt[CTX][:, :], hyc[CTX]["negt"][:, :], dc, writes=["hyc"])
        P.dma("sp", hy_delta[:, :], hy_delta_d[:, :], dc, writes=["hyc"])
        P.copy("dve", identb[:, :], identf[:, :], reads=["identf"], writes=["identb"])
        P.copy("dve", ropeRb[:, :], ropeRf[:, :], reads=["ropeRf"], writes=["ropeRb"])
        P.memset("dve", ones_mean[:, :], 1.0 / D, writes=["ones_mean"])
        P.memset("dve", bd_mean[:, :], 0.0, writes=["bd_mean"])
        P.memset("dve", bd_mean[0:64, 0:64], 1.0 / 64, writes=["bd_mean"])
        P.memset("dve", bd_mean[64:128, 64:128], 1.0 / 64, writes=["bd_mean"])
        P.memset("dve", ones_col[:, :], 1.0, writes=["ones_col"])

        cT = sp("cT").rearrange("p (k j) -> p k j", k=8)
        P.act(scT[:, :, :], cT, AF.Silu, reads=["smallp"], writes=["scT"])
        AR.reset()
        astg = [AR.take([128, 8, 512]) for _ in range(2)]
        ads = [P.dsem("astg0"), P.dsem("astg1")]
        b_ada = sp("b_ada").rearrange("p (l m j) -> p l m j", l=NL, m=48)
        n = 0
        for l in range(LAYERS):
            wv = wview(w_ada, l)
            pm = pbank[l % 2]
            for cb in range(12):
                si = n % 2
                n += 1
                P.dma("sp", astg[si][:, :, :], wv[:, :, cb * 512:(cb + 1) * 512], ads[si], writes=[("astg", si)])
                for mm_ in range(4):
                    m = cb * 4 + mm_
                    for k in range(8):
                        P.mm(pm[:, m * 4:(m + 1) * 4], astg[si][:, k, mm_ * 128:(mm_ + 1) * 128], scT[:, k, :],
                             k == 0, k == 7, reads=[("astg", si), "scT"], writes=[("pmod", l % 2)])
            P.tt("dve", modT[:, l, :, :], pm[:, 0:192].rearrange("p (m j) -> p m j", m=48), b_ada[:, l, :, :], ALU.add,
                 reads=[("pmod", l % 2), "smallp"], writes=["modT"])
        P.barrier()

        norm1 = sp("norm1").rearrange("p (l k) -> p l k", l=NL)
        norm2 = sp("norm2").rearrange("p (l k) -> p l k", l=NL)
        fnorm = sp("final_norm")
        qg = sp("qg")
        kg = sp("kg")
        ffn_conv = sp("ffn_conv").rearrange("p (l j m) -> p l j m", l=NL, j=3)

        def mod_ap(l, which, j):
            return modT[:, l, which * 8:(which + 1) * 8, j]

        def tb_j(tb, s):
            return 2 if tb == 0 else s

        def norm_modulate(l, s, normw, sc_idx, sh_idx, tbs, banks):
            for idx, j in ((0, s), (1, 2)):
                P.stt(AB[:, idx, :], mod_ap(l, sc_idx, j), 1.0, normw[:, l, :], ALU.add, ALU.mult,
                      reads=["modT", "smallp"], writes=["AB"])
            sq = [AR.take([128, 512], BF16) for _ in range(2)]
            rs = [AR.take([128, 512]) for _ in range(2)]
            tmp = [AR.take([128, 512]) for _ in range(2)]
            cnt = 0
            for tb in tbs:
                t0, tl = TBS[tb]
                pbk = banks.next()
                ps = pbank[pbk]
                for k in range(NCH):
                    q = sq[cnt % 2]
                    cnt += 1
                    P.act(q[:, 0:tl], hT[:, k, t0:t0 + tl], AF.Square, reads=[("hT", k, tb)], writes=[("sq", id(q))])
                    P.mm(ps[:, 0:tl], ones_mean[:, :], q[:, 0:tl], k == 0, k == NCH - 1,
                         reads=[("sq", id(q)), "ones_mean"], writes=[("pb", pbk)])
                r = rs[tb % 2]
                P.act(r[:, 0:tl], ps[:, 0:tl], AF.Sqrt, bias=EPS, scale=1.0, reads=[("pb", pbk)], writes=[("rs", tb % 2)])
                P.recip(r[:, 0:tl], r[:, 0:tl], reads=[("rs", tb % 2)], writes=[("rs", tb % 2)])
                ai = 1 if tb == 0 else 0
                jj = tb_j(tb, s)
                for k in range(NCH):
                    tm = tmp[k % 2]
                    P.tt("pool" if k % 2 else "dve", tm[:, 0:tl], hT[:, k, t0:t0 + tl], r[:, 0:tl], ALU.mult,
                         reads=[("hT", k, tb), ("rs", tb % 2)], writes=[("ntmp", k % 2)])
                    P.act(uT[:, k, t0:t0 + tl], tm[:, 0:tl], AF.Identity, scale=AB[:, ai, k:k + 1],
                          bias=mod_ap(l, sh_idx, jj)[:, k:k + 1],
                          reads=[("ntmp", k % 2), "AB", "modT"], writes=[("uT", tb)])

        def wout_accumulate(ws, l, s, mixchunk, src, skey, tbs, banks):
            wt, wk = ws.load([(w_out[l, mixchunk * 128:(mixchunk + 1) * 128, :].rearrange("p (k n) -> p k n", k=8), 0, 128)], 8)
            for m in range(NCH):
                for tb in tbs:
                    t0, tl = TBS[tb]
                    b1 = banks.next()
                    P.mm(pbank[b1][:, 0:tl], wt[:, m, :], src[:, t0:t0 + tl], True, True,
                         reads=[wk, skey], writes=[("pb", b1)])
                    P.stt(hT[:, m, t0:t0 + tl], pbank[b1][:, 0:tl], mod_ap(l, 2, tb_j(tb, s))[:, m:m + 1],
                          hT[:, m, t0:t0 + tl], ALU.mult, ALU.add,
                          reads=[("pb", b1), "modT", ("hT", m, tb)], writes=[("hT", m, tb)])

        dio = P.dsem("io")
        dout = P.dsem("out")
        for s in range(NS):
            P.barrier()
            for k in range(NCH):
                P.dma("sp", hT[:, k, :], h0[s, k * 128:(k + 1) * 128, :], dio, writes=[("hT", k, tb) for tb in range(5)])
            for l in range(LAYERS):
                last = (l == NL - 1)
                tbs_all = list(range(5))
                tbs_out = [1, 2, 3, 4] if last else tbs_all
                wv_in = wview(w_in, l)
                wv_out = wview(w_out, l)

                P.barrier()
                AR.reset()
                norm_modulate(l, s, norm1, 1, 0, tbs_all, Banks(list(range(8))))

                if do_attn:
                    P.barrier()
                    AR.reset()
                    ws = WStream(2, 3)
                    ropec = AR.take([128, LAT], BF16)
                    ropes = AR.take([128, LAT], BF16)
                    rstage = AR.take([128, LAT])
                    dr = P.dsem("rope")
                    for i, dst in enumerate((ropec, ropes)):
                        P.dma("sp", rstage[:, :], ropecs_d[i], dr, writes=["rstage"])
                        P.copy("pool", dst[:, :], rstage[:, :], reads=["rstage"], writes=[("rope", i)])
                    P.barrier()
                    AR.off -= LAT
                    kdup = AR.take([128, 2, T], BF16)
                    vtok = AR.take([128, 18, 128], BF16)
                    qT = AR.take([128, T], BF16)
                    aT = AR.take([128, T], BF16)
                    sqb = [AR.take([128, 512], BF16) for _ in range(2)]
                    rsb = [AR.take([128, 512]) for _ in range(2)]
                    xnb = [AR.take([128, 512], BF16) for _ in range(2)]
                    t1b = [AR.take([128, 512]) for _ in range(2)]
                    pT = [AR.take([128, 512], BF16) for _ in range(4)]
                    rden = AR.take([128, 512])
                    pjb = Banks([6, 7, 0, 1, 2, 3])
                    cntr = [0]

                    def qk_project(wtile, wkey, gain, dst_of_tb, dkey):
                        for tb in tbs_all:
                            t0, tl = TBS[tb]
                            i2 = cntr[0] % 2
                            cntr[0] += 1
                            b1 = pjb.next()
                            ps = pbank[b1]
                            for k in range(NCH):
                                P.mm(ps[:, 0:tl], wtile[:, k, :], uT[:, k, t0:t0 + tl], k == 0, k == NCH - 1,
                                     reads=[wkey, ("uT", tb)], writes=[("pb", b1)])
                            P.act(sqb[i2][:, 0:tl], ps[:, 0:tl], AF.Square, reads=[("pb", b1)], writes=[("sqb", i2)])
                            b2 = pjb.next()
                            ps2 = pbank[b2]
                            P.mm(ps2[:, 0:tl], bd_mean[:, :], sqb[i2][:, 0:tl], True, True,
                                 reads=[("sqb", i2), "bd_mean"], writes=[("pb", b2)])
                            P.act(rsb[i2][:, 0:tl], ps2[:, 0:tl], AF.Sqrt, bias=EPS, scale=1.0,
                                  reads=[("pb", b2)], writes=[("rsb", i2)])
                            P.recip(rsb[i2][:, 0:tl], rsb[i2][:, 0:tl], reads=[("rsb", i2)], writes=[("rsb", i2)])
                            dst = dst_of_tb(t0, tl)
                            if tb == 0:
                                P.stt(dst, ps[:, 0:tl], gain, rsb[i2][:, 0:tl], ALU.mult, ALU.mult,
                                      reads=[("pb", b1), ("rsb", i2), "smallp"], writes=[dkey(tb)])
                                continue
                            P.stt(xnb[i2][:, 0:tl], ps[:, 0:tl], gain, rsb[i2][:, 0:tl], ALU.mult, ALU.mult,
                                  reads=[("pb", b1), ("rsb", i2), "smallp"], writes=[("xnb", i2)])
                            P.mm(ps2[:, 0:tl], ropeRb[:, :], xnb[i2][:, 0:tl], True, True,
                                 reads=[("xnb", i2), "ropeRb"], writes=[("pb", b2)])
                            l0 = t0 - CTX
                            P.tt("pool", t1b[i2][:, 0:tl], xnb[i2][:, 0:tl], ropec[:, l0:l0 + tl], ALU.mult,
                                 reads=[("xnb", i2), ("rope", 0)], writes=[("t1b", i2)])
                            P.tt("dve", xnb[i2][:, 0:tl], ps2[:, 0:tl], ropes[:, l0:l0 + tl], ALU.mult,
                                 reads=[("pb", b2), ("rope", 1)], writes=[("xnb", i2)])
                            P.tt("pool", dst, t1b[i2][:, 0:tl], xnb[i2][:, 0:tl], ALU.add,
                                 reads=[("t1b", i2), ("xnb", i2)], writes=[dkey(tb)])

                    for g in range(2):
                        c0 = 512 + 64 * g
                        wt, wk = ws.load([(wv_in[:, :, c0:c0 + 64], 0, 64), (wv_in[:, :, c0:c0 + 64], 64, 64)], 8)
                        qk_project(wt, wk, kg[:, l:l + 1], lambda t0, tl, g=g: kdup[:, g, t0:t0 + tl], lambda tb, g=g: ("kdup", g, tb))
                    wt, wk = ws.load([(wv_in[:, :, 640:768], 0, 128)], 8)
                    for i in range(18):
                        b1 = pjb.next()
                        ps = pbank[b1]
                        tb = 0 if i < 2 else 1 + (i - 2) // 4
                        for k in range(NCH):
                            P.mm(ps[:, 0:128], uT[:, k, i * 128:(i + 1) * 128], wt[:, k, :], k == 0, k == NCH - 1,
                                 reads=[wk, ("uT", tb)], writes=[("pb", b1)])
                        P.copy("act", vtok[:, i, :], ps[:, 0:128], reads=[("pb", b1)], writes=[("vtok", i)])

                    sbanks = [Banks([0, 1]), Banks([2, 3])]
                    for c in range(4):
                        g = c // 2
                        wt, wk = ws.load([(wv_in[:, :, c * 128:(c + 1) * 128], 0, 128)], 8)
                        qk_project(wt, wk, qg[:, l:l + 1], lambda t0, tl: qT[:, t0:t0 + tl], lambda tb: ("qT", tb))
                        for qb in tbs_out:
                            q0, ql = TBS[qb]
                            kts = [0, 1] if qb == 0 else list(range(18))
                            pend = None
                            for ii, kt in enumerate(kts):
                                ktb = 0 if kt < 2 else 1 + (kt - 2) // 4
                                cur = []
                                for hh in range(2):
                                    bk = sbanks[hh].next()
                                    pr = slice(64 * hh, 64 * hh + 64)
                                    P.mm(pbank[bk][:, 0:ql], kdup[pr, g, kt * 128:(kt + 1) * 128], qT[pr, q0:q0 + ql], True, True,
                                         reads=[("kdup", g, ktb), ("qT", qb)], writes=[("pb", bk)])
                                    cur.append(bk)
                                if pend is not None:
                                    pend()
                                pts = []
                                for hh in range(2):
                                    pi = (2 * ii + hh) % 4
                                    P.act(pT[pi][:, 0:ql], pbank[cur[hh]][:, 0:ql], AF.Exp, scale=0.125,
                                          reads=[("pb", cur[hh])], writes=[("pT", pi)])
                                    pts.append(pi)

                                def pv(kt=kt, pts=pts, first=(ii == 0), lastk=(ii == len(kts) - 1), ql=ql, g=g):
                                    for hh in range(2):
                                        pr = slice(64 * hh, 64 * hh + 64)
                                        P.mm(pbank[4][pr, 0:ql], vtok[:, kt, 64 * g:64 * g + 64], pT[pts[hh]][:, 0:ql], first, lastk,
                                             reads=[("pT", pts[hh]), ("vtok", kt)], writes=[("pb", 4)])
                                        P.mm(pbank[5][pr, 0:ql], ones_col[:, :], pT[pts[hh]][:, 0:ql], first, lastk,
                                             reads=[("pT", pts[hh]), "ones_col"], writes=[("pb", 5)])
                                pend = pv
                            pend()
                            P.recip(rden[:, 0:ql], pbank[5][:, 0:ql], reads=[("pb", 5)], writes=["rden"])
                            P.tt("dve", aT[:, q0:q0 + ql], pbank[4][:, 0:ql], rden[:, 0:ql], ALU.mult,
                                 reads=[("pb", 4), "rden"], writes=[("aT", qb)])
                        for m in range(NCH):
                            pass
                        wt, wk = ws.load([(w_out[l, c * 128:(c + 1) * 128, :].rearrange("p (k n) -> p k n", k=8), 0, 128)], 8)
                        for m in range(NCH):
                            for tb in tbs_out:
                                t0, tl = TBS[tb]
                                b1 = pjb.next()
                                P.mm(pbank[b1][:, 0:tl], wt[:, m, :], aT[:, t0:t0 + tl], True, True,
                                     reads=[wk, ("aT", tb)], writes=[("pb", b1)])
                                P.stt(hT[:, m, t0:t0 + tl], pbank[b1][:, 0:tl], mod_ap(l, 2, tb_j(tb, s))[:, m:m + 1],
                                      hT[:, m, t0:t0 + tl], ALU.mult, ALU.add,
                                      reads=[("pb", b1), "modT", ("hT", m, tb)], writes=[("hT", m, tb)])


                if do_gdn:
                    P.barrier()
                    AR.reset()
                    gdn_conv = sp("gdn_conv").rearrange("p (l j m) -> p l j m", l=NL, j=3)
                    alog_bc = sp("alog_bc").rearrange("p (l w) -> p l w", l=NL)
                    dtb_bc = sp("dtb_bc").rearrange("p (l w) -> p l w", l=NL)
                    gn_bc = sp("gn_bc").rearrange("p (l w) -> p l w", l=NL)
                    segs = [(0, CTX), (CTX, T)]
                    NT = 18
                    beta = AR.take([128, NT, 8])
                    gdec = AR.take([128, NT, 8])
                    gm = AR.take([128, 2816])
                    G0 = AR.off
                    P.dma("sp", gm[:, :], gmask_d[:, :], P.dsem("gmask"), writes=["gm"])
                    TriI = [gm[:, d_ * 768:d_ * 768 + 128] for d_ in range(2)]
                    NegTri = [gm[:, d_ * 768 + 128:d_ * 768 + 256] for d_ in range(2)]
                    MaskB = [gm[:, d_ * 768 + 256:d_ * 768 + 512] for d_ in range(2)]
                    Strict = [gm[:, d_ * 768 + 512:d_ * 768 + 768] for d_ in range(2)]
                    v3 = lambda ap_: ap_.rearrange("p (a b) -> p a b", a=2)
                    BD16 = v3(gm[:, 1536:1792])
                    OFF = {32: v3(gm[:, 1792:2048]), 64: v3(gm[:, 2048:2304]), 128: v3(gm[:, 2304:2560])}
                    I64 = gm[:, 2560:2688]
                    ones128 = gm[:, 2688:2816]
                    ws = WStream(2, 3)
                    bdraw = AR.take([128, NT, 16])
                    tmpa = AR.take([128, NT, 8])
                    tmpb = AR.take([128, NT, 8])
                    expA = AR.take([128, 8])
                    wt, wk = ws.load([(wv_in[:, :, 1792:1808], 0, 16)], 8)
                    gb8 = Banks(list(range(8)))
                    for i in range(NT):
                        b1 = gb8.next()
                        tb = 0 if i < 2 else 1 + (i - 2) // 4
                        for k in range(NCH):
                            P.mm(pbank[b1][:, 0:16], uT[:, k, i * 128:(i + 1) * 128], wt[:, k, 0:16], k == 0, k == NCH - 1,
                                 reads=[wk, ("uT", tb)], writes=[("pb", b1)])
                        P.copy("act", bdraw[:, i, :], pbank[b1][:, 0:16], reads=[("pb", b1)], writes=["bdraw"])
                    P.act(beta[:, :, :], bdraw[:, :, 0:8], AF.Sigmoid, reads=["bdraw"], writes=["beta"])
                    P.tt("dve", tmpa[:, :, :], bdraw[:, :, 8:16], dtb_bc[:, l, :].unsqueeze(1).broadcast_to([128, NT, 8]), ALU.add,
                         reads=["bdraw", "smallp"], writes=["tmpa"])
                    P.act(tmpb[:, :, :], tmpa[:, :, :], AF.Abs, reads=["tmpa"], writes=["tmpb"])
                    P.act(tmpb[:, :, :], tmpb[:, :, :], AF.Exp, scale=-1.0, reads=["tmpb"], writes=["tmpb"])
                    P.act(tmpb[:, :, :], tmpb[:, :, :], AF.Ln, bias=1.0, scale=1.0, reads=["tmpb"], writes=["tmpb"])
                    P.ts("dve", tmpa[:, :, :], tmpa[:, :, :], 0.0, None, ALU.max, reads=["tmpa"], writes=["tmpa"])
                    P.tt("pool", tmpa[:, :, :], tmpa[:, :, :], tmpb[:, :, :], ALU.add, reads=["tmpa", "tmpb"], writes=["tmpa"])
                    P.act(expA[:, :], alog_bc[:, l, :], AF.Exp, reads=["smallp"], writes=["expA"])
                    P.stt(gdec[:, :, :], tmpa[:, :, :], -1.0, expA[:, :].unsqueeze(1).broadcast_to([128, NT, 8]), ALU.mult, ALU.mult,
                          reads=["tmpa", "expA"], writes=["gdec"])

                    gstop = cfg.get("gstop", 0)
                    for pair in (range(2) if gstop != 1 else []):
                        P.barrier()
                        AR.at(G0)
                        qT_ = AR.take([128, T], BF16)
                        kT_ = AR.take([128, T], BF16)
                        vT_ = AR.take([128, T], BF16)
                        Oacc = AR.take([128, NT, 128])
                        G1 = AR.off
                        ws = WStream(2, 3)
                        raw = AR.take([128, T], BF16)
                        acc = AR.take([128, T])
                        sqb = [AR.take([128, 512], BF16) for _ in range(2)]
                        rsb = [AR.take([128, 512]) for _ in range(2)]
                        pjb = Banks([6, 7, 0, 1, 2, 3])
                        sbk = Banks([4, 5])
                        for which, cbase, dst in (("q", 768, qT_), ("k", 1024, kT_), ("v", 1280, vT_)):
                            col0 = cbase + pair * 128
                            mcol = {"q": 0, "k": 2, "v": 4, "gate": 0}[which] + pair
                            wt, wk = ws.load([(wv_in[:, :, col0:col0 + 128], 0, 128)], 8)
                            for tb in tbs_all:
                                t0, tl = TBS[tb]
                                b1 = pjb.next()
                                for k in range(NCH):
                                    P.mm(pbank[b1][:, 0:tl], wt[:, k, :], uT[:, k, t0:t0 + tl], k == 0, k == NCH - 1,
                                         reads=[wk, ("uT", tb)], writes=[("pb", b1)])
                                if which == "gate":
                                    P.act(gsil[:, t0:t0 + tl], pbank[b1][:, 0:tl], AF.Silu, reads=[("pb", b1)], writes=["gsil"])
                                else:
                                    P.copy("act", raw[:, t0:t0 + tl], pbank[b1][:, 0:tl], reads=[("pb", b1)], writes=["graw"])
                            if which == "gate":
                                continue
                            for (a, b) in segs:
                                P.act(acc[:, a:b], raw[:, a:b], AF.Identity, scale=gdn_conv[:, l, 1, mcol:mcol + 1],
                                      reads=["graw", "smallp"], writes=["gacc"])
                                P.stt(acc[:, a + 1:b], raw[:, a:b - 1], gdn_conv[:, l, 0, mcol:mcol + 1], acc[:, a + 1:b], ALU.mult, ALU.add,
                                      reads=["graw", "gacc", "smallp"], writes=["gacc"])
                                P.stt(acc[:, a:b - 1], raw[:, a + 1:b], gdn_conv[:, l, 2, mcol:mcol + 1], acc[:, a:b - 1], ALU.mult, ALU.add,
                                      reads=["graw", "gacc", "smallp"], writes=["gacc"])
                            if which == "v":
                                P.act(vT_[:, :], acc[:, :], AF.Silu, reads=["gacc"], writes=["vT_"])
                                continue
                            P.act(acc[:, :], acc[:, :], AF.Silu, reads=["gacc"], writes=["gacc"])
                            for tb in tbs_all:
                                t0, tl = TBS[tb]
                                i2 = tb % 2
                                P.act(sqb[i2][:, 0:tl], acc[:, t0:t0 + tl], AF.Square, reads=["gacc"], writes=[("gsq", i2)])
                                b2 = sbk.next()
                                P.mm(pbank[b2][:, 0:tl], bd_mean[:, :], sqb[i2][:, 0:tl], True, True,
                                     reads=[("gsq", i2), "bd_mean"], writes=[("pb", b2)])
                                P.act(rsb[i2][:, 0:tl], pbank[b2][:, 0:tl], AF.Sqrt, bias=EPS, scale=64.0,
                                      reads=[("pb", b2)], writes=[("grs", i2)])
                                P.recip(rsb[i2][:, 0:tl], rsb[i2][:, 0:tl], reads=[("grs", i2)], writes=[("grs", i2)])
                                if which == "q":
                                    P.stt(dst[:, t0:t0 + tl], acc[:, t0:t0 + tl], 0.125, rsb[i2][:, 0:tl], ALU.mult, ALU.mult,
                                          reads=["gacc", ("grs", i2)], writes=["qT_"])
                                else:
                                    P.tt("dve", dst[:, t0:t0 + tl], acc[:, t0:t0 + tl], rsb[i2][:, 0:tl], ALU.mult,
                                         reads=["gacc", ("grs", i2)], writes=["kT_"])
                        if pair == 0:
                            dbg_dump("g_beta", beta[:, :, :]); dbg_dump("g_gdec", gdec[:, :, :])
                            dbg_dump("g_qT", qT_[:, :], BF16); dbg_dump("g_kT", kT_[:, :], BF16); dbg_dump("g_vT", vT_[:, :], BF16)
                        if gstop == 2:
                            continue
                        P.barrier()
                        AR.at(G1)

                        def make_set():
                            B = {}
                            B["qkv"] = AR.take([128, 3, 128])
                            B["gc4"] = AR.take([128, 4]); B["egc"] = AR.take([128, 2]); B["erem"] = AR.take([128, 2])
                            B["gend"] = AR.take([128, 2]); B["grem"] = AR.take([128, 2])
                            for nm in ("gB", "ET", "Q0f", "P0f", "Mm", "MT", "Qoff32", "Qoff64", "Poff32", "Poff64", "Poff128", "T1s", "T3s", "Xm", "QKm", "UW", "ZT"):
                                B[nm] = AR.take([128, 2, 128])
                            for nm in ("Kdec", "Qdec", "ATb", "bsb", "S0", "S1"):
                                B[nm] = AR.take([128, 2, 64])
                            return B
                        sets = [make_set(), make_set()]
                        bk = Banks(list(range(8)))
                        P.memset("pool", Oacc[:, :, :], 0.0, writes=[("Oacc", i_) for i_ in range(NT)])

                        def phase_a(i, d, B, z):
                            K_ = lambda n_: (n_, z)
                            hd0 = d * 4 + 2 * pair
                            tsl = slice(i * 128, (i + 1) * 128)
                            qkv, gc4, egc, erem, gend, grem = B["qkv"], B["gc4"], B["egc"], B["erem"], B["gend"], B["grem"]
                            gB, ET, Q0f, P0f, Mm, MT, T1s, T3s, Xm = B["gB"], B["ET"], B["Q0f"], B["P0f"], B["Mm"], B["MT"], B["T1s"], B["T3s"], B["Xm"]
                            Qoff = {32: B["Qoff32"], 64: B["Qoff64"]}
                            Poff = {32: B["Poff32"], 64: B["Poff64"], 128: B["Poff128"]}
                            QKm, UW, ZT, Kdec, Qdec, ATb, bsb = B["QKm"], B["UW"], B["ZT"], B["Kdec"], B["Qdec"], B["ATb"], B["bsb"]
                            b1 = bk.next()
                            for j, src, sk in ((0, qT_, "qT_"), (1, kT_, "kT_"), (2, vT_, "vT_")):
                                P.mm(pbank[b1][:, j * 128:(j + 1) * 128], src[:, tsl], identb[:, :], True, True,
                                     reads=[sk, "identb"], writes=[("pb", b1)])
                            P.copy("act", qkv[:, :, :], pbank[b1][:, 0:384].rearrange("p (a b) -> p a b", a=3), reads=[("pb", b1)], writes=[K_("qkv")])
                            b1 = bk.next()
                            gcol = gdec[:, i, 0:8]
                            P.mm(pbank[b1][:, 0:8], TriI[d], gcol, True, True, reads=["gm", "gdec"], writes=[("pb", b1)])
                            P.mm(pbank[b1][:, 8:16], ones128, gcol, True, True, reads=["gm", "gdec"], writes=[("pb", b1)])
                            P.copy("dve", gc4[:, 0:2], pbank[b1][:, hd0:hd0 + 2], reads=[("pb", b1)], writes=[K_("gc4")])
                            P.copy("dve", gc4[:, 2:4], pbank[b1][:, 8 + hd0:8 + hd0 + 2], reads=[("pb", b1)], writes=[K_("gc4")])
                            yield
                            P.act(egc[:, :], gc4[:, 0:2], AF.Exp, reads=[K_("gc4")], writes=[K_("egc")])
                            P.act(gend[:, :], gc4[:, 2:4], AF.Exp, reads=[K_("gc4")], writes=[K_("gend")])
                            P.tt("dve", grem[:, :], gc4[:, 2:4], gc4[:, 0:2], ALU.subtract, reads=[K_("gc4")], writes=[K_("grem")])
                            P.act(erem[:, :], grem[:, :], AF.Exp, reads=[K_("grem")], writes=[K_("erem")])
                            for h in range(2):
                                P.copy("pool", gB[:, h, :], gdec[:, i, hd0 + h:hd0 + h + 1].broadcast_to([128, 128]), reads=["gdec"], writes=[K_("gB")])
                            b1 = bk.next()
                            for h in range(2):
                                o_ = pbank[b1][:, h * 128:(h + 1) * 128]
                                P.mm(o_, gB[:, h, :], TriI[d], True, False, reads=[K_("gB"), "gm"], writes=[("pb", b1)])
                                P.mm(o_, NegTri[d], gB[:, h, :], False, False, reads=[K_("gB"), "gm"], writes=[("pb", b1)])
                                P.mm(o_, identf[:, :], MaskB[d][:, 0:128], False, True, reads=["identf", "gm"], writes=[("pb", b1)])
                            P.act(ET[:, :, :], v3(pbank[b1][:, 0:256]), AF.Exp, reads=[("pb", b1)], writes=[K_("ET")])
                            gbk = [bk.next(), bk.next()]
                            for h in range(2):
                                pr = slice(64 * h, 64 * h + 64)
                                P.mm(pbank[gbk[h]][:, 0:128], kT_[pr, tsl], kT_[pr, tsl], True, True, reads=["kT_"], writes=[("pb", gbk[h])])
                                P.mm(pbank[gbk[h]][:, 128:256], kT_[pr, tsl], qT_[pr, tsl], True, True,
                                     reads=["kT_", "qT_"], writes=[("pb", gbk[h])])
                            yield
                            for h in range(2):
                                P.tt("dve", QKm[:, h, :], pbank[gbk[h]][:, 128:256], ET[:, h, :], ALU.mult,
                                     reads=[("pb", gbk[h]), K_("ET")], writes=[K_("QKm")])
                            P.tt("pool", ET[:, :, :], ET[:, :, :], v3(Strict[d]), ALU.mult, reads=[K_("ET"), "gm"], writes=[K_("ET")])
                            for h in range(2):
                                P.stt(Q0f[:, h, :], pbank[gbk[h]][:, 0:128], beta[:, i, hd0 + h:hd0 + h + 1], ET[:, h, :], ALU.mult, ALU.mult,
                                      reads=[("pb", gbk[h]), "beta", K_("ET")], writes=[K_("Q0f")])
                            P.copy("pool", Xm[:, :, 0:64], qkv[:, 2, :].rearrange("p (a b) -> p a b", a=2), reads=[K_("qkv")], writes=[K_("Xm")])
                            P.tt("pool", Xm[:, :, 64:128], qkv[:, 1, :].rearrange("p (a b) -> p a b", a=2),
                                 egc[:, :].unsqueeze(2).broadcast_to([128, 2, 64]), ALU.mult, reads=[K_("qkv"), K_("egc")], writes=[K_("Xm")])
                            P.tt("pool", Kdec[:, :, :], qkv[:, 1, :].rearrange("p (a b) -> p a b", a=2),
                                 erem[:, :].unsqueeze(2).broadcast_to([128, 2, 64]), ALU.mult, reads=[K_("qkv"), K_("erem")], writes=[K_("Kdec")])
                            P.tt("pool", Qdec[:, :, :], qkv[:, 0, :].rearrange("p (a b) -> p a b", a=2),
                                 egc[:, :].unsqueeze(2).broadcast_to([128, 2, 64]), ALU.mult, reads=[K_("qkv"), K_("egc")], writes=[K_("Qdec")])
                            yield

                            def mm2(lhs, lk, rhs, rk):
                                bx = bk.next()
                                for h in range(2):
                                    P.mm(pbank[bx][:, h * 128:(h + 1) * 128], lhs[:, h, :], rhs[:, h, :] if rhs is not None else identf[:, :], True, True,
                                         reads=[lk, rk], writes=[("pb", bx)])
                                return bx
                            bx = mm2(Q0f, K_("Q0f"), None, "identf")
                            P.copy("act", P0f[:, :, :], v3(pbank[bx][:, 0:256]), reads=[("pb", bx)], writes=[K_("P0f")])
                            for L_ in (32, 64):
                                P.tt("pool", Qoff[L_][:, :, :], Q0f[:, :, :], OFF[L_], ALU.mult, reads=[K_("Q0f"), "gm"], writes=[K_("Qoff%d" % L_)])
                            P.tt("pool", Q0f[:, :, :], Q0f[:, :, :], BD16, ALU.mult, reads=[K_("Q0f"), "gm"], writes=[K_("Q0f")])
                            yield
                            for L_ in (32, 64, 128):
                                P.tt("pool", Poff[L_][:, :, :], P0f[:, :, :], OFF[L_], ALU.mult, reads=[K_("P0f"), "gm"], writes=[K_("Poff%d" % L_)])
                            P.tt("pool", P0f[:, :, :], P0f[:, :, :], BD16, ALU.mult, reads=[K_("P0f"), "gm"], writes=[K_("P0f")])
                            Qk, Pm = Q0f, P0f
                            kQ, kP = K_("Q0f"), K_("P0f")
                            idb = identf[:, :].unsqueeze(1).broadcast_to([128, 2, 128])
                            P.tt("pool", Mm[:, :, :], idb, Pm[:, :, :], ALU.subtract, reads=["identf", kP], writes=[K_("Mm")])
                            P.tt("pool", MT[:, :, :], idb, Qk[:, :, :], ALU.subtract, reads=["identf", kQ], writes=[K_("MT")])
                            yield
                            for lev in range(1, 4):
                                bp = mm2(Qk, kQ, Pm, kP)
                                bq = mm2(Pm, kP, Qk, kQ)
                                P.copy("act", Pm[:, :, :], v3(pbank[bp][:, 0:256]), reads=[("pb", bp)], writes=[kP])
                                P.copy("dve", Qk[:, :, :], v3(pbank[bq][:, 0:256]), reads=[("pb", bq)], writes=[kQ])
                                yield
                                b1_ = mm2(Qk, kQ, Mm, K_("Mm"))
                                b2_ = mm2(Pm, kP, MT, K_("MT"))
                                P.tt("dve", Mm[:, :, :], Mm[:, :, :], v3(pbank[b1_][:, 0:256]), ALU.add, reads=[("pb", b1_), K_("Mm")], writes=[K_("Mm")])
                                P.tt("dve", MT[:, :, :], MT[:, :, :], v3(pbank[b2_][:, 0:256]), ALU.add, reads=[("pb", b2_), K_("MT")], writes=[K_("MT")])
                                yield
                            for L_ in (32, 64, 128):
                                if L_ < 128:
                                    b1_ = mm2(Qoff[L_], K_("Qoff%d" % L_), Mm, K_("Mm"))
                                    P.copy("act", T1s[:, :, :], v3(pbank[b1_][:, 0:256]), reads=[("pb", b1_)], writes=[K_("T1s")])
                                b3_ = mm2(Poff[L_], K_("Poff%d" % L_), MT, K_("MT"))
                                P.copy("dve" if L_ < 128 else "act", T3s[:, :, :], v3(pbank[b3_][:, 0:256]), reads=[("pb", b3_)], writes=[K_("T3s")])
                                yield
                                if L_ < 128:
                                    b1_ = mm2(MT, K_("MT"), T1s, K_("T1s"))
                                b3_ = mm2(Mm, K_("Mm"), T3s, K_("T3s"))
                                if L_ < 128:
                                    P.tt("dve", Mm[:, :, :], Mm[:, :, :], v3(pbank[b1_][:, 0:256]), ALU.subtract, reads=[("pb", b1_), K_("Mm")], writes=[K_("Mm")])
                                P.tt("dve", MT[:, :, :], MT[:, :, :], v3(pbank[b3_][:, 0:256]), ALU.subtract, reads=[("pb", b3_), K_("MT")], writes=[K_("MT")])
                                yield
                            bx = mm2(MT, K_("MT"), Xm, K_("Xm"))
                            bb = beta[:, i, hd0:hd0 + 2].unsqueeze(2).broadcast_to([128, 2, 64])
                            xps = v3(pbank[bx][:, 0:256])
                            P.tt("dve", UW[:, :, 0:64], xps[:, :, 0:64], bb, ALU.mult, reads=[("pb", bx), "beta"], writes=[K_("UW")])
                            P.stt(UW[:, :, 64:128], xps[:, :, 64:128], -1.0, bb, ALU.mult, ALU.mult, reads=[("pb", bx), "beta"], writes=[K_("UW")])
                            yield
                            ba = bk.next()
                            for h in range(2):
                                P.mm(pbank[ba][0:64, h * 64:(h + 1) * 64], UW[:, h, 64:128], Kdec[:, h, :], True, True,
                                     reads=[K_("UW"), K_("Kdec")], writes=[("pb", ba)])
                            for h in range(2):
                                P.mm(pbank[ba][0:64, 128 + h * 64:128 + (h + 1) * 64], Kdec[:, h, :], UW[:, h, 0:64], True, True,
                                     reads=[K_("UW"), K_("Kdec")], writes=[("pb", ba)])
                            bz = bk.next()
                            for h in range(2):
                                o_ = pbank[bz][0:64, h * 128:(h + 1) * 128]
                                P.mm(o_, Qdec[:, h, :], identf[:, :], True, False, reads=[K_("Qdec"), "identf"], writes=[("pb", bz)])
                                P.mm(o_, UW[:, h, 64:128], QKm[:, h, :], False, True, reads=[K_("UW"), K_("QKm")], writes=[("pb", bz)])
                            for h in range(2):
                                P.stt(ATb[0:64, h, :], I64[0:64, h * 64:(h + 1) * 64], gend[0:64, h:h + 1], pbank[ba][0:64, h * 64:(h + 1) * 64],
                                      ALU.mult, ALU.add, reads=["gm", K_("gend"), ("pb", ba)], writes=[K_("AT")])
                            P.copy("dve", bsb[0:64, :, :], v3(pbank[ba][0:64, 128:256]), reads=[("pb", ba)], writes=[K_("bsb")])
                            P.copy("dve", ZT[0:64, :, :], v3(pbank[bz][0:64, 0:256]), reads=[("pb", bz)], writes=[K_("ZT")])
                            yield

                        def phase_b(i, B, z, sidx, need_out):
                            K_ = lambda n_: (n_, z)
                            S_old, S_new = (B["S0"], B["S1"]) if sidx % 2 == 0 else (B["S1"], B["S0"])
                            ko, kn = K_("S%d" % (sidx % 2)), K_("S%d" % ((sidx + 1) % 2))
                            QKm, UW, ZT, ATb, bsb = B["QKm"], B["UW"], B["ZT"], B["ATb"], B["bsb"]
                            if need_out:
                                bo = bk.next()
                                for h in range(2):
                                    o_ = pbank[bo][:, h * 64:(h + 1) * 64]
                                    P.mm(o_, QKm[:, h, :], UW[:, h, 0:64], True, False, reads=[K_("QKm"), K_("UW")], writes=[("pb", bo)])
                                    P.mm(o_, ZT[0:64, h, :], S_old[0:64, h, :], False, True, reads=[K_("ZT"), ko], writes=[("pb", bo)])
                                P.tt("dve", Oacc[:, i, :], Oacc[:, i, :], pbank[bo][:, 0:128], ALU.add, reads=[("pb", bo), ("Oacc", i)], writes=[("Oacc", i)])
                            bs_ = bk.next()
                            for h in range(2):
                                P.mm(pbank[bs_][0:64, h * 64:(h + 1) * 64], ATb[0:64, h, :], S_old[0:64, h, :], True, True,
                                     reads=[K_("AT"), ko], writes=[("pb", bs_)])
                            P.tt("dve", S_new[0:64, :, :], bsb[0:64, :, :], v3(pbank[bs_][0:64, 0:128]), ALU.add,
                                 reads=[("pb", bs_), K_("bsb")], writes=[kn])
                            yield

                        def thread(d, B, z):
                            order = [0, 1] + list(range(2, NT)) if d == 0 else [1, 0] + list(range(NT - 1, 1, -1))
                            P.memset("dve", B["S0"][:, :, :], 0.0, writes=[("S0", z)])
                            for n_, i in enumerate(order):
                                yield from phase_a(i, d, B, z)
                                yield from phase_b(i, B, z, n_, not (last and i < 2))
                        threads = [thread(0, sets[0], 0), thread(1, sets[1], 1)]
                        alive = [True, True]
                        while any(alive):
                            for z in range(2):
                                if alive[z]:
                                    try:
                                        next(threads[z])
                                    except StopIteration:
                                        alive[z] = False
                        if pair == 0:
                            dbg_dump("g_Oacc", Oacc[:, :, :])
                        if gstop >= 3:
                            continue
                        P.barrier()
                        AR.at(G1)
                        ws = WStream(2, 3)
                        sq = AR.take([128, NT, 128])
                        ssum = AR.take([128, NT, 2])
                        ybf = AR.take([128, NT, 128], BF16)
                        mixG = AR.take([128, T], BF16)
                        gsil = AR.take([128, T], BF16)
                        wt, wk = ws.load([(wv_in[:, :, 1536 + pair * 128:1536 + (pair + 1) * 128], 0, 128)], 8)
                        gpb = Banks([0, 1, 2, 3])
                        for tb in tbs_out:
                            t0, tl = TBS[tb]
                            b1 = gpb.next()
                            for k in range(NCH):
                                P.mm(pbank[b1][:, 0:tl], wt[:, k, :], uT[:, k, t0:t0 + tl], k == 0, k == NCH - 1,
                                     reads=[wk, ("uT", tb)], writes=[("pb", b1)])
                            P.act(gsil[:, t0:t0 + tl], pbank[b1][:, 0:tl], AF.Silu, reads=[("pb", b1)], writes=["gsil"])
                        tiles_o = list(range(2, NT)) if last else list(range(NT))
                        t_lo = tiles_o[0]
                        nto = len(tiles_o)
                        P.tt("pool", sq[:, t_lo:NT, :], Oacc[:, t_lo:NT, :], Oacc[:, t_lo:NT, :], ALU.mult, reads=[("Oacc", i) for i in tiles_o], writes=["gsq2"])
                        red_o = ssum[:, t_lo:NT, :]
                        red_i = sq[:, t_lo:NT, :].rearrange("p t (a b) -> p t a b", a=2)
                        P.op("dve", lambda e, red_o=red_o, red_i=red_i: e.tensor_reduce(out=red_o, in_=red_i, axis=mybir.AxisListType.X, op=ALU.add),
                             reads=["gsq2"], writes=["ssum"])
                        P.act(ssum[:, t_lo:NT, :], ssum[:, t_lo:NT, :], AF.Sqrt, bias=EPS, scale=1.0 / 64, reads=["ssum"], writes=["ssum"])
                        P.recip(ssum[:, t_lo:NT, :], ssum[:, t_lo:NT, :], reads=["ssum"], writes=["ssum"])
                        P.tt("dve", sq[:, t_lo:NT, :].rearrange("p t (a b) -> p t a b", a=2), Oacc[:, t_lo:NT, :].rearrange("p t (a b) -> p t a b", a=2),
                             ssum[:, t_lo:NT, :].unsqueeze(3).broadcast_to([128, nto, 2, 64]), ALU.mult,
                             reads=[("Oacc", i) for i in tiles_o] + ["ssum"], writes=["gsq2"])
                        P.tt("pool", ybf[:, t_lo:NT, :].rearrange("p t (a b) -> p t a b", a=2), sq[:, t_lo:NT, :].rearrange("p t (a b) -> p t a b", a=2),
                             gn_bc[:, l, :].unsqueeze(1).unsqueeze(1).broadcast_to([128, nto, 2, 64]), ALU.mult,
                             reads=["gsq2", "smallp"], writes=["ybf"])
                        fb = Banks([4, 5, 6, 7])
                        for i0 in range(t_lo, NT, 4):
                            b1 = fb.next()
                            n4 = min(4, NT - i0)
                            for ii in range(n4):
                                P.mm(pbank[b1][:, ii * 128:(ii + 1) * 128], ybf[:, i0 + ii, :], identb[:, :], True, True, reads=["ybf", "identb"], writes=[("pb", b1)])
                            P.tt("dve", mixG[:, i0 * 128:(i0 + n4) * 128], pbank[b1][:, 0:n4 * 128], gsil[:, i0 * 128:(i0 + n4) * 128], ALU.mult,
                                 reads=[("pb", b1), "gsil"], writes=["mixG"])
                        if pair == 0:
                            dbg_dump("g_mixG", mixG[:, :], BF16)
                        wout_accumulate(ws, l, s, 4 + pair, mixG, "mixG", tbs_out, Banks([0, 1, 2, 3]))

                if do_hyena:
                    P.barrier()
                    AR.reset()
                    hy_conv = sp("hy_conv").rearrange("p (l j m) -> p l j m", l=NL, j=3)
                    hy_bias = sp("hy_bias").rearrange("p (l m) -> p l m", l=NL)
                    hw1 = sp("hw1").rearrange("p (l w) -> p l w", l=NL)
                    hw2 = sp("hw2").rearrange("p (l w) -> p l w", l=NL)
                    hb1, hb2, hfr = sp("hb1"), sp("hb2"), sp("hfr")
                    segs = ([] if last else [(0, CTX)]) + [(CTX, T)]
                    x0c = AR.take([128, 2, T], BF16)
                    uTh = AR.take([128, 2, T], BF16)
                    R0 = AR.off
                    ws = WStream(2, 3)
                    x1c = AR.take([128, T], BF16)
                    raw = AR.take([128, T], BF16)
                    acc = AR.take([128, T])
                    pjb = Banks(list(range(8)))
                    for cc in range(2):
                        for which, cbase, mbase in (("x1", 2064, 2), ("v", 2320, 4), ("x0", 1808, 0)):
                            col0 = cbase + cc * 128
                            mcol = mbase + cc
                            wt, wk = ws.load([(wv_in[:, :, col0:col0 + 128], 0, 128)], 8)
                            for tb in tbs_out:
                                t0, tl = TBS[tb]
                                b1 = pjb.next()
                                for k in range(NCH):
                                    P.mm(pbank[b1][:, 0:tl], wt[:, k, :], uT[:, k, t0:t0 + tl], k == 0, k == NCH - 1,
                                         reads=[wk, ("uT", tb)], writes=[("pb", b1)])
                                P.copy("act", raw[:, t0:t0 + tl], pbank[b1][:, 0:tl], reads=[("pb", b1)], writes=["hraw"])
                            for (a, b) in segs:
                                P.act(acc[:, a:b], raw[:, a:b], AF.Identity, scale=hy_conv[:, l, 1, mcol:mcol + 1],
                                      reads=["hraw", "smallp"], writes=["hacc"])
                                P.stt(acc[:, a + 1:b], raw[:, a:b - 1], hy_conv[:, l, 0, mcol:mcol + 1], acc[:, a + 1:b], ALU.mult, ALU.add,
                                      reads=["hraw", "hacc", "smallp"], writes=["hacc"])
                                P.stt(acc[:, a:b - 1], raw[:, a + 1:b], hy_conv[:, l, 2, mcol:mcol + 1], acc[:, a:b - 1], ALU.mult, ALU.add,
                                      reads=["hraw", "hacc", "smallp"], writes=["hacc"])
                            a0 = segs[0][0]
                            if which == "x1":
                                P.copy("pool", x1c[:, a0:T], acc[:, a0:T], reads=["hacc"], writes=["x1c"])
                            elif which == "v":
                                P.tt("pool", uTh[:, cc, a0:T], acc[:, a0:T], x1c[:, a0:T], ALU.mult, reads=["hacc", "x1c"], writes=[("uTh", cc)])
                            else:
                                P.copy("pool", x0c[:, cc, a0:T], acc[:, a0:T], reads=["hacc"], writes=[("x0c", cc)])
                    def sin_reduce(dst, arg, ki, kf, n, key):
                        P.ts("dve", ki[0:64, 0:n], arg[0:64, 0:n], 1.0 / (2.0 * np.pi), None, ALU.mult, reads=[key], writes=[key + "ki"])
                        P.copy("pool", kf[0:64, 0:n], ki[0:64, 0:n], reads=[key + "ki"], writes=[key + "kf"])
                        P.stt(arg[0:64, 0:n], kf[0:64, 0:n], -2.0 * np.pi, arg[0:64, 0:n], ALU.mult, ALU.add,
                              reads=[key + "kf", key], writes=[key])
                        P.act(dst[0:64, 0:n], arg[0:64, 0:n], AF.Sin, scale=0.999999, reads=[key], writes=[key + "out"])

                    for (L_, toff, seg0) in ([(LAT, 2, CTX)] + ([] if last else [(CTX, 0, 0)])):
                        hc_ = hyc[L_]
                        ntc = L_ // 128
                        nfc = ntc
                        tbw = min(512, L_)
                        ntb = L_ // tbw
                        P.barrier()
                        AR.at(R0)
                        u_tok = AR.take([128, 18, 256], BF16)
                        R1 = AR.off
                        tiles = list(range(2, 18)) if L_ == LAT else [0, 1]
                        tb4 = Banks([0, 1, 2, 3])
                        for i in tiles:
                            b1 = tb4.next()
                            for cc in range(2):
                                P.mm(pbank[b1][:, cc * 128:(cc + 1) * 128], uTh[:, cc, i * 128:(i + 1) * 128], identb[:, :], True, True,
                                     reads=[("uTh", cc), "identb"], writes=[("pb", b1)])
                            P.copy("act" if i % 2 else "dve", u_tok[:, i, :], pbank[b1][:, 0:256], reads=[("pb", b1)], writes=[("u_tok", i)])
                        P.barrier()
                        AR.at(R1)
                        S_tok = AR.take([128, 16, 256], BF16)
                        D_tok = AR.take([128, 16, 256], BF16)
                        R2 = AR.off
                        hw3 = AR.take([128, 512])
                        P.dma("sp", hw3[0:64, :], hw3_d[l], P.dsem("hyft"), writes=["hw3"])
                        ft = AR.take([128, 512])
                        P.memset("dve", ft[:, :], 0.0, writes=["ft"])
                        arg = AR.take([128, 512])
                        ki = AR.take([128, 512]).bitcast(I32)
                        kf = AR.take([128, 512])
                        hid1 = AR.take([128, 512])
                        hid2 = AR.take([128, 512])
                        win = AR.take([128, 256])
                        hf = AR.take([128, 256])
                        hb = AR.take([128, 256])
                        dft_ = P.dsem("hyft")
                        fb4 = Banks([0, 1, 2, 3])
                        BW = tbw
                        for blk in range(L_ // BW):
                            P.dma("sp", ft[0:33, 0:BW], hc_["featsT"][:, blk * BW:(blk + 1) * BW], dft_, writes=["ft"])
                            b1 = fb4.next()
                            P.mm(pbank[b1][0:64, 0:BW], hw1[0:33, l, :], ft[0:33, 0:BW], True, True, reads=["ft", "smallp"], writes=[("pb", b1)])
                            P.ts("dve", arg[0:64, 0:BW], pbank[b1][0:64, 0:BW], hb1[0:64, l:l + 1], hfr[0:64, l:l + 1], ALU.add, ALU.mult,
                                 reads=[("pb", b1), "smallp"], writes=["harg"])
                            sin_reduce(hid1, arg, ki, kf, BW, "harg")
                            b1 = fb4.next()
                            P.mm(pbank[b1][0:64, 0:BW], hw2[0:64, l, :], hid1[0:64, 0:BW], True, True, reads=["hargout", "smallp"], writes=[("pb", b1)])
                            P.ts("dve", arg[0:64, 0:BW], pbank[b1][0:64, 0:BW], hb2[0:64, l:l + 1], hfr[0:64, l:l + 1], ALU.add, ALU.mult,
                                 reads=[("pb", b1), "smallp"], writes=["harg"])
                            sin_reduce(hid2, arg, ki, kf, BW, "harg")
                            for tt_ in range(BW // 128):
                                tcg = blk * (BW // 128) + tt_
                                b1 = fb4.next()
                                P.mm(pbank[b1][:, 0:512], hid2[0:64, tt_ * 128:(tt_ + 1) * 128], hw3[0:64, :], True, True,
                                     reads=["hargout", "hw3"], writes=[("pb", b1)])
                                P.act(win[:, :], hy_delta[:, :], AF.Exp, scale=hy_negt[L_][:, tcg:tcg + 1], reads=["hyc"], writes=["win"])
                                P.tt("dve", hf[:, :], pbank[b1][:, 0:256], win[:, :], ALU.mult, reads=[("pb", b1), "win"], writes=["hf"])
                                P.tt("dve", hb[:, :], pbank[b1][:, 256:512], win[:, :], ALU.mult, reads=[("pb", b1), "win"], writes=["hb"])
                                if tcg == 0:
                                    P.memset("dve", hb[0:1, :], 0.0, writes=["hb"])
                                P.tt("pool", S_tok[:, tcg, :], hf[:, :], hb[:, :], ALU.add, reads=["hf", "hb"], writes=[("S_tok", tcg)])
                                P.tt("pool", D_tok[:, tcg, :], hf[:, :], hb[:, :], ALU.subtract, reads=["hf", "hb"], writes=[("D_tok", tcg)])
                        dbg_dump("u_tok%d" % L_, u_tok[:, :, :], BF16)
                        dbg_dump("S_tok%d" % L_, S_tok[:, :, :], BF16)
                        dbg_dump("D_tok%d" % L_, D_tok[:, :, :], BF16)
                        P.barrier()
                        AR.at(R2)
                        Yre = AR.take([128, nfc, 256], BF16)
                        Yim = AR.take([128, nfc, 256], BF16)
                        R3 = AR.off
                        cb = [AR.take([128, ntc, 128], BF16) for _ in range(2)]
                        sbt = [AR.take([128, ntc, 128], BF16) for _ in range(2)]
                        abs_ = AR.take([128, 512])
                        tm = [AR.take([128, 256]) for _ in range(2)]
                        dfw = [P.dsem("hyfw0"), P.dsem("hyfw1")]
                        for fc in range(nfc):
                            bi = fc % 2
                            P.dma("sp", cb[bi][:, :, :], hc_["fwdC"][fc], dfw[bi], writes=[("cb", bi)])
                            P.dma("sp", sbt[bi][:, :, :], hc_["fwdS"][fc], dfw[bi], writes=[("sbt", bi)])
                            bpq = 0 + 2 * bi
                            bab = 1 + 2 * bi
                            for (bk_, c0_, mat_, mk_, rhs_, rk_) in ((bpq, 0, cb, "cb", u_tok, "u_tok"), (bpq, 256, sbt, "sbt", u_tok, "u_tok"),
                                                                  (bab, 0, cb, "cb", S_tok, "S_tok"), (bab, 256, sbt, "sbt", D_tok, "D_tok")):
                                for tc in range(ntc):
                                    ti = toff + tc if rk_ == "u_tok" else tc
                                    P.mm(pbank[bk_][:, c0_:c0_ + 256], mat_[bi][:, tc, :], rhs_[:, ti, :], tc == 0, tc == ntc - 1,
                                         reads=[(mk_, bi), (rk_, ti)], writes=[("pb", bk_)])
                            P.copy("act", abs_[:, :], pbank[bab][:, :], reads=[("pb", bab)], writes=["abs"])
                            Pp, Qp = pbank[bpq][:, 0:256], pbank[bpq][:, 256:512]
                            As, Bs = abs_[:, 0:256], abs_[:, 256:512]
                            P.tt("dve", tm[0][:, :], Pp, As, ALU.mult, reads=[("pb", bpq), "abs"], writes=[("tm", 0)])
                            P.tt("dve", tm[1][:, :], Qp, Bs, ALU.mult, reads=[("pb", bpq), "abs"], writes=[("tm", 1)])
                            P.tt("pool", Yre[:, fc, :], tm[0][:, :], tm[1][:, :], ALU.subtract, reads=[("tm", 0), ("tm", 1)], writes=[("Y", fc)])
                            P.tt("dve", tm[0][:, :], Pp, Bs, ALU.mult, reads=[("pb", bpq), "abs"], writes=[("tm", 0)])
                            P.tt("dve", tm[1][:, :], Qp, As, ALU.mult, reads=[("pb", bpq), "abs"], writes=[("tm", 1)])
                            P.tt("pool", Yim[:, fc, :], tm[0][:, :], tm[1][:, :], ALU.add, reads=[("tm", 0), ("tm", 1)], writes=[("Y", fc)])
                        dbg_dump("Yre%d" % L_, Yre[:, :, :], BF16)
                        dbg_dump("Yim%d" % L_, Yim[:, :, :], BF16)
                        P.barrier()
                        AR.at(R3)
                        FG = min(4, nfc)
                        ci = [AR.take([128, FG, tbw], BF16) for _ in range(2)]
                        si = [AR.take([128, FG, tbw], BF16) for _ in range(2)]
                        ytmp = [AR.take([128, 512]) for _ in range(2)]
                        AR.at(R0)
                        ws = WStream(2, 3)
                        aTh = [AR.take([128, T], BF16) for _ in range(2)]
                        assert AR.off <= R2
                        div = [P.dsem("hyiv0"), P.dsem("hyiv1")]
                        nld = 0
                        for tb_ in range(ntb):
                            banks = (4 + 2 * (tb_ % 2), 5 + 2 * (tb_ % 2))
                            for fg in range(nfc // FG):
                                bi = nld % 2
                                nld += 1
                                P.dma("sp", ci[bi][:, :, :], hc_["invC"][tb_, :, fg * FG:(fg + 1) * FG, :], div[bi], writes=[("ci", bi)])
                                P.dma("sp", si[bi][:, :, :], hc_["invS"][tb_, :, fg * FG:(fg + 1) * FG, :], div[bi], writes=[("si", bi)])
                                for fi in range(FG):
                                    fc = fg * FG + fi
                                    for cc in range(2):
                                        P.mm(pbank[banks[cc]][:, 0:tbw], Yre[:, fc, cc * 128:(cc + 1) * 128], ci[bi][:, fi, :], fc == 0, False,
                                             reads=[("Y", fc), ("ci", bi)], writes=[("pb", banks[cc])])
                                        P.mm(pbank[banks[cc]][:, 0:tbw], Yim[:, fc, cc * 128:(cc + 1) * 128], si[bi][:, fi, :], False, fc == nfc - 1,
                                             reads=[("Y", fc), ("si", bi)], writes=[("pb", banks[cc])])
                            t0 = seg0 + tb_ * tbw
                            for cc in range(2):
                                yt = ytmp[cc]
                                P.stt(yt[:, 0:tbw], uTh[:, cc, t0:t0 + tbw], hy_bias[:, l, cc:cc + 1], pbank[banks[cc]][:, 0:tbw], ALU.mult, ALU.add,
                                      reads=[("uTh", cc), "smallp", ("pb", banks[cc])], writes=[("ytmp", cc)])
                                P.tt("pool", aTh[cc][:, t0:t0 + tbw], yt[:, 0:tbw], x0c[:, cc, t0:t0 + tbw], ALU.mult,
                                     reads=[("ytmp", cc), ("x0c", cc)], writes=[("aTh", cc)])
                        dbg_dump("aTh%d" % L_, aTh[0][:, :], BF16)
                        dbg_dump("x0c%d" % L_, x0c[:, :, :], BF16)
                        dbg_dump("uTh%d" % L_, uTh[:, :, :], BF16)
                        obk = Banks([0, 1, 2, 3])
                        tbs_seg = [1, 2, 3, 4] if L_ == LAT else [0]
                        for cc in range(2):
                            wout_accumulate(ws, l, s, 6 + cc, aTh[cc], ("aTh", cc), tbs_seg, obk)

                if do_ffn:
                    P.barrier()
                    AR.reset()
                    norm_modulate(l, s, norm2, 4, 3, tbs_out, Banks(list(range(8))))
                    P.barrier()
                    AR.reset()
                    ws = WStream(3, 4)
                    JG = 4
                    hid = AR.take([128, JG, T], BF16)
                    rawg = AR.take([128, T], BF16)
                    rawv = AR.take([128, T], BF16)
                    accg = AR.take([128, T])
                    accv = AR.take([128, T])
                    wd = [AR.take([128, JG, D], BF16) for _ in range(1)]
                    wv_up = wview(ffn_up, l)
                    upb = Banks([0, 1, 2, 3])
                    dnb = Banks([4, 5, 6, 7])
                    segs = ([] if last else [(0, CTX)]) + [(CTX, T)]
                    for j0 in range(0, NJ, JG):
                        js = list(range(j0, min(NJ, j0 + JG)))
                        for jj, j in enumerate(js):
                            wg, wgk = ws.load([(wv_up[:, :, j * 128:(j + 1) * 128], 0, 128)], 8)
                            wvv, wvk = ws.load([(wv_up[:, :, DFF + j * 128:DFF + (j + 1) * 128], 0, 128)], 8)
                            for tb in tbs_out:
                                t0, tl = TBS[tb]
                                for (wt, wk, raw, rk) in ((wg, wgk, rawg, "rawg"), (wvv, wvk, rawv, "rawv")):
                                    b1 = upb.next()
                                    for k in range(NCH):
                                        P.mm(pbank[b1][:, 0:tl], wt[:, k, :], uT[:, k, t0:t0 + tl], k == 0, k == NCH - 1,
                                             reads=[wk, ("uT", tb)], writes=[("pb", b1)])
                                    P.copy("act", raw[:, t0:t0 + tl], pbank[b1][:, 0:tl], reads=[("pb", b1)], writes=[rk])
                            for (raw, rk, acc, ak, mcol) in ((rawg, "rawg", accg, "accg", j), (rawv, "rawv", accv, "accv", NJ + j)):
                                w0 = ffn_conv[:, l, 0, mcol:mcol + 1]
                                w1 = ffn_conv[:, l, 1, mcol:mcol + 1]
                                w2 = ffn_conv[:, l, 2, mcol:mcol + 1]
                                for (a, b) in segs:
                                    P.act(acc[:, a:b], raw[:, a:b], AF.Identity, scale=w1, reads=[rk, "smallp"], writes=[ak])
                                    P.stt(acc[:, a + 1:b], raw[:, a:b - 1], w0, acc[:, a + 1:b], ALU.mult, ALU.add,
                                          reads=[rk, ak, "smallp"], writes=[ak])
                                    P.stt(acc[:, a:b - 1], raw[:, a + 1:b], w2, acc[:, a:b - 1], ALU.mult, ALU.add,
                                          reads=[rk, ak, "smallp"], writes=[ak])
                            a0 = segs[0][0]
                            P.act(accg[:, a0:T], accg[:, a0:T], AF.Silu, reads=["accg"], writes=["accg"])
                            P.tt("pool", hid[:, jj, a0:T], accg[:, a0:T], accv[:, a0:T], ALU.mult,
                                 reads=["accg", "accv"], writes=[("hid", jj)])
                        nj = len(js)
                        dtl = wd[0]
                        for m in range(NCH):
                            pass
                        for jj, j in enumerate(js):
                            wt, wk = ws.load([(ffn_down[l, j * 128:(j + 1) * 128, :].rearrange("p (k n) -> p k n", k=8), 0, 128)], 8)
                            P.copy("pool", dtl[:, jj, :].rearrange("p (k n) -> p k n", k=8), wt[:, :, :], reads=[wk], writes=[("wd", jj)])
                        for m in range(NCH):
                            for tb in tbs_out:
                                t0, tl = TBS[tb]
                                b1 = dnb.next()
                                for jj in range(nj):
                                    P.mm(pbank[b1][:, 0:tl], dtl[:, jj, m * 128:(m + 1) * 128], hid[:, jj, t0:t0 + tl], jj == 0, jj == nj - 1,
                                         reads=[("wd", jj), ("hid", jj)], writes=[("pb", b1)])
                                P.stt(hT[:, m, t0:t0 + tl], pbank[b1][:, 0:tl], mod_ap(l, 5, tb_j(tb, s))[:, m:m + 1],
                                      hT[:, m, t0:t0 + tl], ALU.mult, ALU.add,
                                      reads=[("pb", b1), "modT", ("hT", m, tb)], writes=[("hT", m, tb)])

            P.barrier()
            AR.reset()
            if raw_out:
                for k in range(NCH):
                    P.dma("sp", out_d[s, k * 128:(k + 1) * 128, :], hT[:, k, :], dout, reads=[("hT", k, tb) for tb in range(5)])
            else:
                sq = [AR.take([128, 512], BF16) for _ in range(2)]
                rs = [AR.take([128, 512]) for _ in range(2)]
                ob = [AR.take([128, 512]) for _ in range(4)]
                fb = Banks([6, 7])
                cnt = 0
                oc = 0
                for tb in (1, 2, 3, 4):
                    t0, tl = TBS[tb]
                    pbk = fb.next()
                    ps = pbank[pbk]
                    for k in range(NCH):
                        q = sq[cnt % 2]
                        cnt += 1
                        P.act(q[:, 0:tl], hT[:, k, t0:t0 + tl], AF.Square, reads=[("hT", k, tb)], writes=[("sq", id(q))])
                        P.mm(ps[:, 0:tl], ones_mean[:, :], q[:, 0:tl], k == 0, k == NCH - 1,
                             reads=[("sq", id(q)), "ones_mean"], writes=[("pb", pbk)])
                    r = rs[tb % 2]
                    P.act(r[:, 0:tl], ps[:, 0:tl], AF.Sqrt, bias=EPS, scale=1.0, reads=[("pb", pbk)], writes=[("rs", tb % 2)])
                    P.recip(r[:, 0:tl], r[:, 0:tl], reads=[("rs", tb % 2)], writes=[("rs", tb % 2)])
                    for k in range(NCH):
                        o = ob[oc % 4]
                        P.stt(o[:, 0:tl], hT[:, k, t0:t0 + tl], fnorm[:, k:k + 1], r[:, 0:tl], ALU.mult, ALU.mult,
                              reads=[("hT", k, tb), ("rs", tb % 2), "smallp"], writes=[("ob", oc % 4)])
                        P.dma("sp", out_d[s, k * 128:(k + 1) * 128, t0 - CTX:t0 - CTX + tl], o[:, 0:tl], dout, reads=[("ob", oc % 4)])
                        oc += 1
        P.emit(final_waits=[dout, ddbg])
    return nc


CFG = {}


def prepare(inputs, cfg):
    NS = cfg.get("NS", 2)
    ncores = cfg.get("ncores", 8)
    x = np.asarray(inputs["x"], np.float32)
    ctx = np.asarray(inputs["ctx"], np.float32)
    c = np.asarray(inputs["c"], np.float32)
    c_ctx = np.asarray(inputs["c_ctx"], np.float32)
    cos, sin, RT = rope_tables()
    shared = {
        "w_ada": np.ascontiguousarray(inputs["w_ada"], dtype=np.float32),
        "w_in": np.ascontiguousarray(inputs["w_in"], dtype=np.float32),
        "w_out": np.ascontiguousarray(inputs["w_out"], dtype=np.float32),
        "ffn_up": np.ascontiguousarray(inputs["ffn_up"], dtype=np.float32),
        "ffn_down": np.ascontiguousarray(inputs["ffn_down"], dtype=np.float32),
        "ropecs": np.stack([cos, sin]).astype(np.float32),
        "ropeR": RT,
        "ident": np.eye(128, dtype=np.float32),
        "hy_delta": hyena_deltas(),
        "gmask": gdn_masks(),
        "hyena_w3": np.ascontiguousarray(inputs["hyena_w3"], dtype=np.float32),
    }
    for L_ in (LAT, CTX):
        for k_, v_ in hyena_consts(L_).items():
            shared["hy_%s%d" % (k_, L_)] = v_
    in_maps = []
    cols = None
    for core in range(ncores):
        bs = [core * NS + i for i in range(NS)]
        h0 = np.stack([np.ascontiguousarray(np.concatenate([ctx[b], x[b]], axis=0).T) for b in bs])
        cv = np.stack([c[bs[0]], c[bs[-1]], c_ctx])
        pk = small_layout(inputs, cv)
        cols = pk.cols
        m = dict(shared)
        m["h0"] = h0
        m["smallp"] = pk.pack()
        in_maps.append(m)
    return in_maps, cols


def kernel(**inputs):
    cfg = dict(CFG)
    NS = cfg.get("NS", 2)
    ncores = cfg.get("ncores", 8)
    in_maps, cols = prepare(inputs, cfg)
    cfg["nsmall"] = in_maps[0]["smallp"].shape[1]
    nc = build(cfg, cols)
    res = run_bass_kernel_spmd(nc, in_maps, core_ids=list(range(ncores)))
    outs = []
    for core in range(ncores):
        o = res.results[core]["out"]
        for i in range(NS):
            outs.append(np.ascontiguousarray(o[i].T))
    return np.stack(outs).astype(np.float32)
```
